# Optimizing a Trainium2 kernel written in Bass

```python
import jax
import jax.numpy as jnp
from jax import lax
import numpy as np

D_MODEL = 1024
BATCH = 8
SEQ = 2048
DEPTH = 2

GRID_W = 64
CTX_LEN = 256
CONV_DIM = 512
CONV_WIDTH = 31
N_HEADS = 8
N_KV_HEADS = 2
HEAD_DIM = 64
ATTN_DIM = N_HEADS * HEAD_DIM
KV_DIM = N_KV_HEADS * HEAD_DIM
WINDOW = 128
BLOCK = 128
ROPE_BASE = 10000.0
SGU_DIM = 512
SGU_GROUPS = 8
SGU_CHUNK = 128
N_BRANCH = 3
FFN_DIM = 2816
FFN_CONV_WIDTH = 3
EPS = 1e-6
NEG_INF = -1e30
IN_SPLITS = (CONV_DIM, CONV_DIM, ATTN_DIM, KV_DIM, KV_DIM, SGU_DIM, SGU_DIM, N_BRANCH * D_MODEL)
IN_DIM = 2 * CONV_DIM + ATTN_DIM + 2 * KV_DIM + 2 * SGU_DIM + N_BRANCH * D_MODEL
KV_START = 2 * CONV_DIM + ATTN_DIM

kernel_name = 'hybrid_conv_swa_gmlp_dit_block'


def rms_norm(x, g):
    xf = x.astype(jnp.float32)
    y = xf * lax.rsqrt(jnp.mean(xf * xf, axis=-1, keepdims=True) + EPS)
    return (y * g.astype(jnp.float32)).astype(x.dtype)


def layer_norm(x, g, b):
    xf = x.astype(jnp.float32)
    mu = jnp.mean(xf, axis=-1, keepdims=True)
    var = jnp.mean(jnp.square(xf - mu), axis=-1, keepdims=True)
    y = (xf - mu) * lax.rsqrt(var + EPS)
    return (y * g.astype(jnp.float32) + b.astype(jnp.float32)).astype(x.dtype)


def modulation(cond, w, b):
    m = jax.nn.silu(cond) @ w + b
    m = m.reshape(-1, 1, 6, D_MODEL)
    return [m[:, :, i] for i in range(6)]


def modulate(h, shift, scale):
    return h * (1.0 + scale) + shift


def dwconv(x, w, b):
    k = w.shape[0]
    y = lax.conv_general_dilated(
        x, w[:, None, :], window_strides=(1,), padding=[(k // 2, k // 2)],
        dimension_numbers=('NWC', 'WIO', 'NWC'), feature_group_count=x.shape[-1])
    return y + b


def rope_axis(x, pos):
    half = x.shape[-1] // 2
    inv = jnp.power(ROPE_BASE, -jnp.arange(half, dtype=jnp.float32) / half)
    ang = pos.astype(jnp.float32)[:, None] * inv[None, :]
    cos = jnp.cos(ang)[:, None, :]
    sin = jnp.sin(ang)[:, None, :]
    xf = x.astype(jnp.float32)
    x1, x2 = xf[..., :half], xf[..., half:]
    return jnp.concatenate([x1 * cos - x2 * sin, x2 * cos + x1 * sin], axis=-1).astype(x.dtype)


def rope_2d(x, rows, cols):
    n = x.shape[-1] // 2
    return jnp.concatenate([rope_axis(x[..., :n], rows), rope_axis(x[..., n:], cols)], axis=-1)


def split_in(z):
    bounds = np.cumsum(IN_SPLITS)[:-1].tolist()
    return jnp.split(z, bounds, axis=-1)


def qk_heads(t, n_heads, g):
    return rms_norm(t.reshape(*t.shape[:-1], n_heads, HEAD_DIM), g)


def window_attention(q, k, v, kc, vc, sink):
    bsz, s = q.shape[0], q.shape[1]
    nb = s // BLOCK
    grp = N_HEADS // N_KV_HEADS
    scale = HEAD_DIM ** -0.5
    qb = q.reshape(bsz, nb, BLOCK, N_KV_HEADS, grp, HEAD_DIM)
    pad = ((0, 0), (BLOCK, BLOCK), (0, 0), (0, 0))
    kp = jnp.pad(k, pad).reshape(bsz, nb + 2, BLOCK, N_KV_HEADS, HEAD_DIM)
    vp = jnp.pad(v, pad).reshape(bsz, nb + 2, BLOCK, N_KV_HEADS, HEAD_DIM)
    kb = jnp.concatenate([kp[:, :-2], kp[:, 1:-1], kp[:, 2:]], axis=2)
    vb = jnp.concatenate([vp[:, :-2], vp[:, 1:-1], vp[:, 2:]], axis=2)
    s_loc = jnp.einsum('bnqhgd,bnkhd->bnhgqk', qb, kb).astype(jnp.float32) * scale
    s_ctx = jnp.einsum('bnqhgd,bchd->bnhgqc', qb, kc).astype(jnp.float32) * scale
    r = jnp.arange(BLOCK)[:, None]
    j = jnp.arange(3 * BLOCK)[None, :]
    in_window = jnp.abs(j - BLOCK - r) <= WINDOW
    kpos = (jnp.arange(nb)[:, None] - 1) * BLOCK + jnp.arange(3 * BLOCK)[None, :]
    in_range = (kpos >= 0) & (kpos < s)
    mask = in_window[None] & in_range[:, None, :]
    s_loc = jnp.where(mask[None, :, None, None], s_loc, NEG_INF)
    sink_l = jnp.broadcast_to(
        sink.astype(jnp.float32).reshape(N_KV_HEADS, grp)[None, None, :, :, None, None],
        s_loc.shape[:-1] + (1,))
    p = jax.nn.softmax(jnp.concatenate([s_loc, s_ctx, sink_l], axis=-1), axis=-1)
    nl = 3 * BLOCK
    nc = kc.shape[1]
    p_loc = p[..., :nl].astype(v.dtype)
    p_ctx = p[..., nl:nl + nc].astype(v.dtype)
    o = (jnp.einsum('bnhgqk,bnkhd->bnqhgd', p_loc, vb)
         + jnp.einsum('bnhgqc,bchd->bnqhgd', p_ctx, vc))
    return o.reshape(bsz, s, ATTN_DIM)


def context_attention(q, k, v, sink):
    bsz, n = q.shape[0], q.shape[1]
    grp = N_HEADS // N_KV_HEADS
    qg = q.reshape(bsz, n, N_KV_HEADS, grp, HEAD_DIM)
    s = jnp.einsum('bqhgd,bkhd->bhgqk', qg, k).astype(jnp.float32) * (HEAD_DIM ** -0.5)
    sink_l = jnp.broadcast_to(
        sink.astype(jnp.float32).reshape(N_KV_HEADS, grp)[None, :, :, None, None],
        s.shape[:-1] + (1,))
    p = jax.nn.softmax(jnp.concatenate([s, sink_l], axis=-1), axis=-1)[..., :n]
    o = jnp.einsum('bhgqk,bkhd->bqhgd', p.astype(v.dtype), v)
    return o.reshape(bsz, n, ATTN_DIM)


def conv_module(a, b, p):
    y = a * jax.nn.sigmoid(b)
    y = dwconv(y, p['conv_dw_w'], p['conv_dw_b'])
    y = jax.nn.silu(layer_norm(y, p['conv_ln_g'], p['conv_ln_b']))
    return y @ p['conv_out']


def sgu_module(u, v, p):
    u = jax.nn.gelu(u)
    v = layer_norm(jax.nn.gelu(v), p['sgu_ln_g'], p['sgu_ln_b'])
    bsz, n = v.shape[0], v.shape[1]
    vc = v.reshape(bsz, n // SGU_CHUNK, SGU_CHUNK, SGU_GROUPS, SGU_DIM // SGU_GROUPS)
    mixed = jnp.einsum('gpq,bnqgc->bnpgc', p['sgu_w'], vc) + p['sgu_b'].T[:, :, None]
    y = u * mixed.reshape(bsz, n, SGU_DIM)
    return y @ p['sgu_out']


def mixer_merge(z, attn, p):
    a_conv, b_conv, _, _, _, u, v, g = z
    y_conv = conv_module(a_conv, b_conv, p)
    y_attn = attn @ p['attn_out']
    y_sgu = sgu_module(u, v, p)
    gates = jax.nn.sigmoid(g.reshape(*g.shape[:-1], N_BRANCH, D_MODEL) + p['gate_b'])
    m = gates[..., 0, :] * y_conv + gates[..., 1, :] * y_attn + gates[..., 2, :] * y_sgu
    return m @ p['w_o']


def conv_ffn(h, p):
    z = h @ p['ffn_up']
    z = dwconv(z, p['ffn_dw_w'], p['ffn_dw_b'])
    a, b = jnp.split(z, 2, axis=-1)
    return (jax.nn.silu(a) * b) @ p['ffn_down']


def setup_inputs(seed: int = 0) -> dict:
    key = jax.random.key(seed)
    ks = jax.random.split(key, 32)

    def nrm(i, shape, scale):
        return jax.random.normal(ks[i], shape, jnp.float32) * scale

    L = DEPTH
    D = D_MODEL
    return {
        'x': nrm(0, (BATCH, SEQ, D), 1.0),
        'c': nrm(1, (BATCH, D), 1.0),
        'ctx': nrm(2, (BATCH, CTX_LEN, D), 1.0),
        'c_ctx': nrm(3, (D,), 1.0),
        'ada_w': nrm(4, (L, D, 6 * D), 0.5 * D ** -0.5),
        'ada_b': nrm(5, (L, 6 * D), 0.01),
        'norm1_g': 1.0 + nrm(6, (L, D), 0.05),
        'norm2_g': 1.0 + nrm(7, (L, D), 0.05),
        'w_in': nrm(8, (L, D, IN_DIM), D ** -0.5),
        'gate_b': nrm(9, (L, N_BRANCH, D), 0.01),
        'conv_dw_w': nrm(10, (L, CONV_WIDTH, CONV_DIM), CONV_WIDTH ** -0.5),
        'conv_dw_b': nrm(11, (L, CONV_DIM), 0.01),
        'conv_ln_g': 1.0 + nrm(12, (L, CONV_DIM), 0.05),
        'conv_ln_b': nrm(13, (L, CONV_DIM), 0.01),
        'conv_out': nrm(14, (L, CONV_DIM, D), CONV_DIM ** -0.5),
        'q_norm_g': 1.0 + nrm(15, (L, HEAD_DIM), 0.05),
        'k_norm_g': 1.0 + nrm(16, (L, HEAD_DIM), 0.05),
        'attn_sink': nrm(17, (L, N_HEADS), 0.5),
        'attn_out': nrm(18, (L, ATTN_DIM, D), ATTN_DIM ** -0.5),
        'sgu_ln_g': 1.0 + nrm(19, (L, SGU_DIM), 0.05),
        'sgu_ln_b': nrm(20, (L, SGU_DIM), 0.01),
        'sgu_w': nrm(21, (L, SGU_GROUPS, SGU_CHUNK, SGU_CHUNK), SGU_CHUNK ** -0.5),
        'sgu_b': nrm(22, (L, SGU_GROUPS, SGU_CHUNK), 0.01),
        'sgu_out': nrm(23, (L, SGU_DIM, D), SGU_DIM ** -0.5),
        'w_o': nrm(24, (L, D, D), D ** -0.5),
        'ffn_up': nrm(25, (L, D, 2 * FFN_DIM), D ** -0.5),
        'ffn_dw_w': nrm(26, (L, FFN_CONV_WIDTH, 2 * FFN_DIM), FFN_CONV_WIDTH ** -0.5),
        'ffn_dw_b': nrm(27, (L, 2 * FFN_DIM), 0.01),
        'ffn_down': nrm(28, (L, FFN_DIM, D), FFN_DIM ** -0.5),
    }


def reference(x, c, ctx, c_ctx, ada_w, ada_b, norm1_g, norm2_g, w_in, gate_b,
              conv_dw_w, conv_dw_b, conv_ln_g, conv_ln_b, conv_out,
              q_norm_g, k_norm_g, attn_sink, attn_out,
              sgu_ln_g, sgu_ln_b, sgu_w, sgu_b, sgu_out, w_o,
              ffn_up, ffn_dw_w, ffn_dw_b, ffn_down):
    s = x.shape[1]
    rows_n = s // GRID_W
    rows = jnp.repeat(jnp.arange(rows_n), GRID_W)
    cols = jnp.tile(jnp.arange(GRID_W), rows_n)
    for l in range(DEPTH):
        last = l == DEPTH - 1
        p = {
            'gate_b': gate_b[l], 'w_o': w_o[l],
            'conv_dw_w': conv_dw_w[l], 'conv_dw_b': conv_dw_b[l],
            'conv_ln_g': conv_ln_g[l], 'conv_ln_b': conv_ln_b[l], 'conv_out': conv_out[l],
            'attn_out': attn_out[l],
            'sgu_ln_g': sgu_ln_g[l], 'sgu_ln_b': sgu_ln_b[l], 'sgu_w': sgu_w[l],
            'sgu_b': sgu_b[l], 'sgu_out': sgu_out[l],
            'ffn_up': ffn_up[l], 'ffn_dw_w': ffn_dw_w[l], 'ffn_dw_b': ffn_dw_b[l],
            'ffn_down': ffn_down[l],
        }
        sx1, cx1, gx1, sx2, cx2, gx2 = modulation(c, ada_w[l], ada_b[l])
        sc1, cc1, gc1, sc2, cc2, gc2 = modulation(c_ctx[None, :], ada_w[l], ada_b[l])

        hx = modulate(rms_norm(x, norm1_g[l]), sx1, cx1)
        hc = modulate(rms_norm(ctx, norm1_g[l]), sc1, cc1)

        if last:
            kv_c = hc @ w_in[l][:, KV_START:KV_START + 2 * KV_DIM]
            kc_flat, vc_flat = jnp.split(kv_c, 2, axis=-1)
        else:
            zc = split_in(hc @ w_in[l])
            kc_flat, vc_flat = zc[3], zc[4]
        kc = qk_heads(kc_flat, N_KV_HEADS, k_norm_g[l])
        vc = vc_flat.reshape(*vc_flat.shape[:-1], N_KV_HEADS, HEAD_DIM)

        zx = split_in(hx @ w_in[l])
        qx = rope_2d(qk_heads(zx[2], N_HEADS, q_norm_g[l]), rows, cols)
        kx = rope_2d(qk_heads(zx[3], N_KV_HEADS, k_norm_g[l]), rows, cols)
        vx = zx[4].reshape(*zx[4].shape[:-1], N_KV_HEADS, HEAD_DIM)
        ax = window_attention(qx, kx, vx, kc, vc, attn_sink[l])
        x_new = x + gx1 * mixer_merge(zx, ax, p)
        x_new = x_new + gx2 * conv_ffn(modulate(rms_norm(x_new, norm2_g[l]), sx2, cx2), p)

        if not last:
            qc = qk_heads(zc[2], N_HEADS, q_norm_g[l])
            ac = context_attention(qc, kc, vc, attn_sink[l])
            ctx = ctx + gc1 * mixer_merge(zc, ac, p)
            ctx = ctx + gc2 * conv_ffn(modulate(rms_norm(ctx, norm2_g[l]), sc2, cc2), p)
        x = x_new
    return x
```

```python
import numpy as np
from contextlib import ExitStack
import concourse.bass as bass
import concourse.mybir as mybir
from concourse.bass_utils import run_bass_kernel_spmd

F32 = mybir.dt.float32
BF16 = mybir.dt.bfloat16
AF = mybir.ActivationFunctionType
ALU = mybir.AluOpType
AX = mybir.AxisListType

D = 1024
S_LEN = 2048
CL = 256
T = S_LEN + CL
DEPTH = 2
EPS = 1e-6
NS = 418
PARTS = [(0, 768), (768, 1536), (1536, 2304)]
HW = 772
YW = 2352
NDS = 8
NWS = 5

O_ADAB, O_G1, O_G2, O_GATEB, O_CONVW, O_CONVB, O_CLNG, O_CLNB = 0, 48, 56, 64, 88, 212, 216, 220
O_QG, O_KG, O_SINK, O_SGUB, O_FFNW, O_FFNB = 224, 225, 226, 234, 242, 374
C_ID, C_RMS, C_LN, C_BLK, C_ROT, C_TPREV, C_TNEXT = 0, 1, 2, 3, 4, 5, 6


def groups_of(p0, p1):
    out = []
    t = p0
    while t < p1:
        lim = p1 if (t >= S_LEN or p1 <= S_LEN) else S_LEN
        n = min(384, lim - t)
        out.append((t, t + n))
        t += n
    return out


ALL_GROUPS = [g for p in PARTS for g in groups_of(*p)]
GIDX = {g: i for i, g in enumerate(ALL_GROUPS)}


def gidx_of_token(t):
    for i, (a, b) in enumerate(ALL_GROUPS):
        if a <= t < b:
            return i
    raise ValueError(t)


def hpos(pi, t):
    p0, _ = PARTS[pi]
    if pi < 2:
        return t - p0 + 1
    return (t - 1536 + 1) if t < S_LEN else (t - S_LEN + 515)


def ypos(t):
    return t + 16 if t < S_LEN else t + 32


class Slot:
    __slots__ = ("w", "r")

    def __init__(self):
        self.w = {}
        self.r = {}


class Op:
    __slots__ = ("eng", "fn", "dma", "sig", "id", "deps", "dk", "semval", "dsem", "dval", "ep")


class Sched:
    ENG = ["pe", "act", "dve", "pool", "sp"]

    def __init__(self):
        self.ops = {e: [] for e in self.ENG}
        self.dma_hist = {e: [] for e in self.ENG}
        self.last = {e: None for e in self.ENG}
        self.barrier_ops = []
        self.dma_since_barrier = []
        self.nid = 0
        self.epoch = 0

    def op(self, eng, fn, reads=(), writes=(), dma=False):
        o = Op()
        o.eng, o.fn, o.dma, o.sig, o.id = eng, fn, dma, False, self.nid
        o.ep = self.epoch
        self.nid += 1
        deps = {}
        key = ("dma", o.id) if dma else eng

        def add(p, raw):
            if (not p.dma) and (not dma) and p.eng == eng:
                if (not raw) or eng == "pe":
                    return
            deps[p.id] = p

        for s in reads:
            for p in s.w.values():
                add(p, True)
        for s in writes:
            for p in s.w.values():
                add(p, False)
            for p in s.r.values():
                add(p, False)
        for p in self.barrier_ops:
            add(p, True)
        if dma:
            k = len(self.dma_hist[eng])
            if k >= NDS:
                prev = self.dma_hist[eng][k - NDS]
                deps[prev.id] = prev
            o.dk = k
            self.dma_hist[eng].append(o)
            self.dma_since_barrier.append(o)
        o.deps = list(deps.values())
        for p in o.deps:
            p.sig = True
        wset = set(id(s) for s in writes)
        for s in writes:
            if s.r:
                s.w = {key: o}
                s.r = {}
            else:
                s.w[key] = o
        for s in reads:
            if id(s) not in wset:
                s.r[key] = o
        self.ops[eng].append(o)
        if not dma:
            self.last[eng] = o
        return o

    def barrier(self):
        self.barrier_ops = [o for o in self.last.values() if o is not None] + list(self.dma_since_barrier)
        self.dma_since_barrier = []
        self.epoch += 1

    def alloc_sems(self, nc, es):
        self.csem = {}
        self.dsems = {}
        for e in self.ENG:
            if self.ops[e]:
                for ep in range(self.epoch + 1):
                    self.csem[(e, ep)] = es.enter_context(nc.semaphore("c_%s%d" % (e, ep)))
                if self.dma_hist[e]:
                    self.dsems[e] = [es.enter_context(nc.semaphore("d_%s%d" % (e, i))) for i in range(NDS)]

    def emit(self, nc, block, final_waits):
        csem = self.csem
        dsems = self.dsems
        for e in self.ENG:
            c = {}
            for o in self.ops[e]:
                if o.dma:
                    o.dsem = dsems[e][o.dk % NDS]
                    o.dval = 16 * (o.dk // NDS + 1)
                elif o.sig:
                    c[o.ep] = c.get(o.ep, 0) + 1
                    o.semval = c[o.ep]

        def run(e, h):
            known = {}

            def wait_for(p):
                if p.dma:
                    sem, val, key = p.dsem, p.dval, ("d", p.eng, p.dk % NDS)
                else:
                    sem, val, key = csem[(p.eng, p.ep)], p.semval, ("c", p.eng, p.ep)
                if known.get(key, 0) >= val:
                    return
                h.wait_ge(sem, val)
                known[key] = val

            for o in self.ops[e]:
                for p in o.deps:
                    wait_for(p)
                ins = o.fn(h)
                if o.dma:
                    ins.then_inc(o.dsem, 16)
                elif o.sig:
                    ins.then_inc(csem[(e, o.ep)], 1)
            for p in final_waits.get(e, []):
                wait_for(p)

        hooks = {"pe": block.tensor, "act": block.scalar, "dve": block.vector, "pool": block.gpsimd, "sp": block.sync}
        for e in self.ENG:
            if self.ops[e]:
                hooks[e](lambda h, e=e: run(e, h))


class Ring:
    def __init__(self, aps):
        self.aps = aps
        self.slots = [Slot() for _ in aps]
        self.i = 0

    def get(self):
        i = self.i
        self.i = (i + 1) % len(self.aps)
        return self.aps[i], self.slots[i]


def build_program(layer_ids, dbg=None):
    L = len(layer_ids)
    nc = bass.Bass("TRN2", target_bir_lowering=False)
    S = Sched()

    def din(name, shape):
        return nc.dram_tensor(name, list(shape), F32, kind="ExternalInput").ap()

    xin = din("xT_in", [128, 8, T])
    ccin = din("cc", [128, 8, 2])
    adain = din("ada", [L, 24, 128, 2, 8, 128])
    smin = din("smalls", [L, 128, NS])
    sglnin = din("sgln", [L, 128, 2, 512])
    sguwin = din("sguw", [L, 128, 8, 128])
    cstin = din("consts", [128, 7, 128])
    cosin = din("cos", [128, S_LEN])
    sinin = din("sin", [128, S_LEN])
    WAin = din("WA", [L, 5, 128, 2, 8, 128])
    WQin = din("WQ", [L, 4, 128, 8, 128])
    WUVin = din("WUV", [L, 4, 128, 8, 256])
    WGin = din("WG", [L, 8, 3, 128, 8, 128])
    WBin = din("WB", [L, 8, 128, 3, 4, 128])
    WOin = din("WO", [L, 8, 128, 8, 128])
    WUPin = din("WUP", [L, 22, 128, 2, 8, 128])
    WDNin = din("WDN", [L, 8, 2, 128, 11, 128])
    xout = nc.dram_tensor("xT_out", [128, 8, T], F32, kind="ExternalOutput").ap()
    dbg_outs = {}

    es = ExitStack()

    def sb(name, shape, dt):
        return es.enter_context(nc.sbuf_tensor(name, list(shape), dt))

    xT = sb("xT", [128, 8, T], F32)
    hT = sb("hT", [128, 8, HW], BF16)
    kT = sb("kT", [128, T], BF16)
    Vaug = sb("Vaug", [128, 18, 2, 65], BF16)
    CT_W = 4 * T
    X_W = 4 * YW + 31 * 128
    assert X_W >= 4 * 768 * 2 + 8 * 768
    AR_W = CT_W + X_W
    assert AR_W >= 22 * 768
    arena = sb("arena", [128, AR_W], BF16)
    cT = arena[:, 0:CT_W].rearrange("p (c t) -> p c t", c=4)
    yT = arena[:, CT_W:CT_W + 4 * YW].rearrange("p (c t) -> p c t", c=4)
    Dg = arena[:, CT_W + 4 * YW:CT_W + 4 * YW + 31 * 128].rearrange("p (k c) -> p k c", k=31)
    attnT = arena[:, CT_W:CT_W + 4 * 768].rearrange("p (c t) -> p c t", c=4)
    sguT = arena[:, CT_W + 4 * 768:CT_W + 8 * 768].rearrange("p (c t) -> p c t", c=4)
    mT = arena[:, CT_W + 8 * 768:CT_W + 16 * 768].rearrange("p (c t) -> p c t", c=8)
    gT = arena[:, 0:22 * 768].rearrange("p (c t) -> p c t", c=22)

    wsl = sb("wsl", [128, NWS, 2048], BF16)
    wring = Ring([wsl[:, i, :] for i in range(NWS)])
    Ft = sb("Ft", [128, 6, 512], F32)
    fring = Ring([Ft[:, i, :] for i in range(6)])
    Rt = sb("Rt", [128, 2, 384], F32)
    rring = Ring([Rt[:, i, :] for i in range(2)])
    Bt = sb("Bt", [128, 6, 512], BF16)
    bring = Ring([Bt[:, i, :] for i in range(6)])
    sqb = sb("sqb", [128, 8, 384], BF16)
    sq_slot = Slot()
    qTc = sb("qTc", [128, 2, 768], BF16)
    qring = Ring([qTc[:, i, :] for i in range(2)])
    PTt = sb("PT", [128, 2, 5, 128], BF16)
    pring = Ring([PTt[:, i, :, :] for i in range(2)])
    cst = sb("cst", [128, 7, 128], BF16)
    smalls = sb("smalls_sb", [128, L, NS], F32)
    sgln = sb("sgln_sb", [128, 2, 512], F32)
    sguw = sb("sguw_sb", [128, 8, 128], BF16)
    cct = sb("cct", [128, 8, 2], F32)
    sct = sb("sct", [128, 8, 2], BF16)
    modT = sb("modT", [128, 6, 8, 2], F32)
    AB = sb("AB", [128, 2, 8, 2], F32)
    esk = sb("esk", [128, 8], F32)
    halo_buf = sb("halo_buf", [128, 8, 4], BF16)
    halo_slot = Slot()
    tiny = sb("tiny", [128, 8, 8], F32)
    tring = Ring([tiny[:, i, :] for i in range(8)])

    ps = [es.enter_context(nc.psum_tensor("ps%d" % i, [128, 512], F32)) for i in range(7)]
    psring = Ring([p[:, :] for p in ps])
    psb = es.enter_context(nc.psum_tensor("psb", [128, 512], BF16))
    psb_slot = Slot()

    xs = [[Slot() for _ in ALL_GROUPS] for _ in range(8)]
    hs = [Slot() for _ in range(8)]
    ks = [Slot() for _ in ALL_GROUPS]
    vs = [Slot() for _ in range(18)]
    ys = [Slot() for _ in range(4)]
    cs = [[Slot() for _ in ALL_GROUPS] for _ in range(4)]
    dg_slot = Slot()
    ats = [Slot() for _ in range(4)]
    sgs = [Slot() for _ in range(4)]
    ms = [Slot() for _ in range(8)]
    gs = [Slot() for _ in range(22)]
    cst_slot, sm_slot, sgln_slot, sguw_slot = Slot(), Slot(), Slot(), Slot()
    cc_slot, sct_slot, mod_slot, ab_slot, esk_slot = Slot(), Slot(), Slot(), Slot(), Slot()

    def sm(li, off, n=1):
        return smalls[:, li, off:off + n]

    def ACT(out, in_, func, r, w, bias=None, scale=None):
        kw = {}
        if bias is not None:
            kw["bias"] = bias
        if scale is not None:
            kw["scale"] = scale
        return S.op("act", lambda e: e.activation(out=out, in_=in_, func=func, **kw), r, w)

    def TT(out, in0, in1, op, r, w):
        return S.op("dve", lambda e: e.tensor_tensor(out=out, in0=in0, in1=in1, op=op), r, w)

    def TS(out, in0, s1, op0, r, w, s2=None, op1=None):
        if op1 is None:
            return S.op("dve", lambda e: e.tensor_scalar(out=out, in0=in0, scalar1=s1, scalar2=None, op0=op0), r, w)
        return S.op("dve", lambda e: e.tensor_scalar(out=out, in0=in0, scalar1=s1, scalar2=s2, op0=op0, op1=op1), r, w)

    def STT(out, in0, scalar, in1, op0, op1, r, w):
        return S.op("dve", lambda e: e.scalar_tensor_tensor(out=out, in0=in0, scalar=scalar, in1=in1, op0=op0, op1=op1), r, w)

    def RECIP(ap, slot):
        return S.op("dve", lambda e: e.reciprocal(out=ap, in_=ap), [slot], [slot])

    def MM(out, pairs, r, w):
        n = len(pairs)
        o = None
        for i, (l, rh) in enumerate(pairs):
            o = S.op("pe", lambda e, l=l, rh=rh, i=i: e.matmul(out, l, rh, start=(i == 0), stop=(i == n - 1)), r, w)
        return o

    def WDMA(view_shape, src, nelem):
        ap, slot = wring.get()
        v = ap[:, 0:nelem]
        if len(view_shape) == 3:
            v = v.rearrange("p (a b) -> p a b", a=view_shape[1])
        elif len(view_shape) == 4:
            v = v.rearrange("p (a b c) -> p a b c", a=view_shape[1], b=view_shape[2])
        S.op("pool", lambda e: e.dma_start(out=v, in_=src), [], [slot], dma=True)
        return v, slot

    def PDMA(out, src, w):
        return S.op("sp", lambda e: e.dma_start(out=out, in_=src), [], w, dma=True)

    for k in range(8):
        PDMA(xT[:, k, :], xin[:, k, :], xs[k])
    S.op("pool", lambda e: e.dma_start(out=cst[:], in_=cstin), [], [cst_slot], dma=True)
    PDMA(smalls[:], smin.rearrange("l p n -> p l n"), [sm_slot])
    PDMA(cct[:], ccin, [cc_slot])
    ACT(sct[:], cct[:], AF.Silu, [cc_slot], [sct_slot])
    S.op("pool", lambda e: e.memset(Vaug[:, :, :, 64:65], 1.0), [], vs)

    def ident():
        return cst[:, C_ID, :]

    def stage_mod(li):
        pst, pslot = psring.get()
        for jg in range(24):
            wv, wslot = WDMA([128, 2, 8, 128], adain[li, jg], 2048)
            for jj in range(2):
                j = jg * 2 + jj
                MM(pst[:, j * 2:(j + 1) * 2], [(wv[:, jj, k, :], sct[:, k, :]) for k in range(8)], [wslot, sct_slot], [pslot])
        pv = pst[:, 0:96].rearrange("p (j s) -> p j s", s=2)
        mv = modT[:].rearrange("p i k s -> p (i k) s")
        for s_ in range(2):
            TT(mv[:, :, s_], pv[:, :, s_], sm(li, O_ADAB, 48), ALU.add, [pslot, sm_slot], [mod_slot])
        for which, (mi, og) in enumerate([(1, O_G1), (4, O_G2)]):
            for s_ in range(2):
                STT(AB[:, which, :, s_], modT[:, mi, :, s_], 1.0, sm(li, og, 8), ALU.add, ALU.mult, [mod_slot, sm_slot], [ab_slot])
        ACT(esk[:], sm(li, O_SINK, 8), AF.Exp, [sm_slot], [esk_slot])

    def norm_cols(li, which, pi, t0, t1):
        n = t1 - t0
        s_ = 1 if t0 >= S_LEN else 0
        hp0 = hpos(pi, t0)
        rs = []
        a = t0
        while a < t1:
            g = gidx_of_token(a)
            rs.append(g)
            a = ALL_GROUPS[g][1]
        xr = [xs[k][g] for k in range(8) for g in rs]
        ACT(sqb[:, :, 0:n], xT[:, :, t0:t1], AF.Square, xr, [sq_slot])
        pst, pslot = psring.get()
        MM(pst[:, 0:n], [(cst[:, C_RMS, :], sqb[:, k, 0:n]) for k in range(8)], [sq_slot, cst_slot], [pslot])
        sd, sdslot = rring.get()
        ACT(sd[:, 0:n], pst[:, 0:n], AF.Sqrt, [pslot], [sdslot], bias=EPS)
        RECIP(sd[:, 0:n], sdslot)
        shift_i = 0 if which == 0 else 3
        for k in range(8):
            tmp, tslot = fring.get()
            STT(tmp[:, 0:n], xT[:, k, t0:t1], AB[:, which, k, s_:s_ + 1], sd[:, 0:n], ALU.mult, ALU.mult,
                [xs[k][g] for g in rs] + [ab_slot, sdslot], [tslot])
            ACT(hT[:, k, hp0:hp0 + n], tmp[:, 0:n], AF.Identity, [tslot, mod_slot], [hs[k]],
                bias=modT[:, shift_i, k, s_:s_ + 1])

    def part_groups(pi, li_is_last, phase_b):
        gl = groups_of(*PARTS[pi])
        if phase_b and li_is_last:
            gl = [g for g in gl if g[0] < S_LEN]
        return gl

    def stage_norm(li, which, pi, last, with_halo):
        gl = groups_of(*PARTS[pi])
        if last and which == 1:
            gl = [g for g in gl if g[0] < S_LEN]
        for (t0, t1) in gl:
            norm_cols(li, which, pi, t0, t1)
        if with_halo:
            def zero_col(c):
                S.op("dve", lambda e: e.memset(hT[:, :, c:c + 1], 0.0), [], hs)
            p0, p1 = PARTS[pi]
            x1 = min(p1, S_LEN)
            if p0 > 0:
                S.op("dve", lambda e, pi=pi: e.tensor_copy(out=hT[:, :, 0:1], in_=halo_buf[:, :, pi:pi + 1]), [halo_slot], hs)
            else:
                zero_col(0)
            if x1 < S_LEN:
                save = hpos(pi, x1 - 1) + 1
                norm_cols_at(li, which, pi, x1, save)
            else:
                zero_col(hpos(pi, x1 - 1) + 1)
            if pi == 2:
                zero_col(514)
                zero_col(771)

    def norm_cols_at(li, which, pi, t, col, to_halo=None):
        g = gidx_of_token(t)
        xr = [xs[k][g] for k in range(8)]
        ACT(sqb[:, :, 0:1], xT[:, :, t:t + 1], AF.Square, xr, [sq_slot])
        pst, pslot = psring.get()
        MM(pst[:, 0:1], [(cst[:, C_RMS, :], sqb[:, k, 0:1]) for k in range(8)], [sq_slot, cst_slot], [pslot])
        sd, sdslot = rring.get()
        ACT(sd[:, 0:1], pst[:, 0:1], AF.Sqrt, [pslot], [sdslot], bias=EPS)
        RECIP(sd[:, 0:1], sdslot)
        shift_i = 0 if which == 0 else 3
        for k in range(8):
            tmp, tslot = fring.get()
            STT(tmp[:, 0:1], xT[:, k, t:t + 1], AB[:, which, k, 0:1], sd[:, 0:1], ALU.mult, ALU.mult,
                [xs[k][g], ab_slot, sdslot], [tslot])
            if to_halo is None:
                ACT(hT[:, k, col:col + 1], tmp[:, 0:1], AF.Identity, [tslot, mod_slot], [hs[k]],
                    bias=modT[:, shift_i, k, 0:1])
            else:
                ACT(halo_buf[:, k, to_halo:to_halo + 1], tmp[:, 0:1], AF.Identity, [tslot, mod_slot], [halo_slot],
                    bias=modT[:, shift_i, k, 0:1])

    def qknorm_rope(li, praw, pslot, n, goff, dst, dslots, rope, t0):
        sq, sqs = bring.get()
        ACT(sq[:, 0:n], praw[:, 0:n], AF.Square, [pslot], [sqs])
        p2, p2s = psring.get()
        MM(p2[:, 0:n], [(cst[:, C_BLK, :], sq[:, 0:n])], [sqs, cst_slot], [p2s])
        sd, sds = rring.get()
        ACT(sd[:, 0:n], p2[:, 0:n], AF.Sqrt, [p2s], [sds], bias=EPS)
        RECIP(sd[:, 0:n], sds)
        if not rope:
            STT(dst, praw[:, 0:n], sm(li, goff), sd[:, 0:n], ALU.mult, ALU.mult, [pslot, sm_slot, sds], dslots)
            return
        qn, qns = bring.get()
        STT(qn[:, 0:n], praw[:, 0:n], sm(li, goff), sd[:, 0:n], ALU.mult, ALU.mult, [pslot, sm_slot, sds], [qns])
        p3, p3s = psring.get()
        MM(p3[:, 0:n], [(cst[:, C_ROT, :], qn[:, 0:n])], [qns, cst_slot], [p3s])
        cs_, css = fring.get()
        sn_, sns = fring.get()
        PDMA(cs_[:, 0:n], cosin[:, t0:t0 + n], [css])
        PDMA(sn_[:, 0:n], sinin[:, t0:t0 + n], [sns])
        TT(cs_[:, 0:n], qn[:, 0:n], cs_[:, 0:n], ALU.mult, [qns, css], [css])
        TT(sn_[:, 0:n], p3[:, 0:n], sn_[:, 0:n], ALU.mult, [p3s, sns], [sns])
        TT(dst, cs_[:, 0:n], sn_[:, 0:n], ALU.add, [css, sns], dslots)

    def stage_proj_a(li, pi, last):
        gl = groups_of(*PARTS[pi])
        for c in range(4):
            wv, wslot = WDMA([128, 2, 8, 128], WAin[li, c], 2048)
            for (t0, t1) in gl:
                if last and t0 >= S_LEN:
                    continue
                n = t1 - t0
                hp0 = hpos(pi, t0)
                pa, pas = psring.get()
                pb, pbs = psring.get()
                MM(pa[:, 0:n], [(wv[:, 0, k, :], hT[:, k, hp0:hp0 + n]) for k in range(8)], [wslot] + hs, [pas])
                MM(pb[:, 0:n], [(wv[:, 1, k, :], hT[:, k, hp0:hp0 + n]) for k in range(8)], [wslot] + hs, [pbs])
                sg, sgs_ = fring.get()
                ACT(sg[:, 0:n], pb[:, 0:n], AF.Sigmoid, [pbs], [sgs_])
                yp0 = ypos(t0)
                TT(yT[:, c, yp0:yp0 + n], pa[:, 0:n], sg[:, 0:n], ALU.mult, [pas, sgs_], [ys[c]])
        wv, wslot = WDMA([128, 2, 8, 128], WAin[li, 4], 2048)
        for (t0, t1) in gl:
            n = t1 - t0
            hp0 = hpos(pi, t0)
            gi = GIDX[(t0, t1)]
            pk, pks = psring.get()
            MM(pk[:, 0:n], [(wv[:, 0, k, :], hT[:, k, hp0:hp0 + n]) for k in range(8)], [wslot] + hs, [pks])
            qknorm_rope(li, pk, pks, n, O_KG, kT[:, t0:t1], [ks[gi]], t0 < S_LEN, t0)
            for tt in range(t0 // 128, t1 // 128):
                hq = hpos(pi, tt * 128)
                pv, pvs = psring.get()
                MM(pv[:, 0:128], [(hT[:, k, hq:hq + 128], wv[:, 1, k, :]) for k in range(8)], [wslot] + hs, [pvs])
                ACT(Vaug[:, tt, :, 0:64], pv[:, 0:128].rearrange("p (g d) -> p g d", g=2), AF.Copy, [pvs], [vs[tt]])

    def stage_conv(li, last):
        gl = [g for g in ALL_GROUPS if not (last and g[0] >= S_LEN)]
        for c in range(4):
            for k in range(31):
                TS(Dg[:, k, :], ident(), sm(li, O_CONVW + c * 31 + k), ALU.mult, [cst_slot, sm_slot], [dg_slot])
            for (t0, t1) in gl:
                n = t1 - t0
                yp0 = ypos(t0)
                pc, pcs = psring.get()
                MM(pc[:, 0:n], [(Dg[:, k, :], yT[:, c, yp0 + k - 15:yp0 + k - 15 + n]) for k in range(31)],
                   [dg_slot, ys[c]], [pcs])
                ACT(cT[:, c, t0:t1], pc[:, 0:n], AF.Identity, [pcs, sm_slot], [cs[c][GIDX[(t0, t1)]]],
                    bias=sm(li, O_CONVB + c))
        for (t0, t1) in gl:
            n = t1 - t0
            gi = GIDX[(t0, t1)]
            cr = [cs[c][gi] for c in range(4)]
            ACT(sqb[:, 0:4, 0:n], cT[:, :, t0:t1], AF.Square, cr, [sq_slot])
            pm, pms = psring.get()
            pe_, pes = psring.get()
            MM(pm[:, 0:n], [(cst[:, C_LN, :], cT[:, c, t0:t1]) for c in range(4)], cr + [cst_slot], [pms])
            MM(pe_[:, 0:n], [(cst[:, C_LN, :], sqb[:, c, 0:n]) for c in range(4)], [sq_slot, cst_slot], [pes])
            mS, mSs = fring.get()
            vr, vrs = fring.get()
            ACT(mS[:, 0:n], pm[:, 0:n], AF.Copy, [pms], [mSs])
            ACT(vr[:, 0:n], pm[:, 0:n], AF.Square, [pms], [vrs])
            TT(vr[:, 0:n], pe_[:, 0:n], vr[:, 0:n], ALU.subtract, [pes, vrs], [vrs])
            ACT(vr[:, 0:n], vr[:, 0:n], AF.Sqrt, [vrs], [vrs], bias=EPS)
            RECIP(vr[:, 0:n], vrs)
            for c in range(4):
                t_, ts_ = fring.get()
                TT(t_[:, 0:n], cT[:, c, t0:t1], mS[:, 0:n], ALU.subtract, [cs[c][gi], mSs], [ts_])
                TT(t_[:, 0:n], t_[:, 0:n], vr[:, 0:n], ALU.mult, [ts_, vrs], [ts_])
                ACT(cT[:, c, t0:t1], t_[:, 0:n], AF.Silu, [ts_, sm_slot], [cs[c][gi]],
                    bias=sm(li, O_CLNB + c), scale=sm(li, O_CLNG + c))

    def stage_attn(li, pi, last):
        gl = part_groups(pi, last, True)
        base = PARTS[pi][0]
        for c in range(4):
            wv, wslot = WDMA([128, 8, 128], WQin[li, c], 1024)
            qT, qslot = qring.get()
            for (t0, t1) in gl:
                n = t1 - t0
                hp0 = hpos(pi, t0)
                pq, pqs = psring.get()
                MM(pq[:, 0:n], [(wv[:, k, :], hT[:, k, hp0:hp0 + n]) for k in range(8)], [wslot] + hs, [pqs])
                qknorm_rope(li, pq, pqs, n, O_QG, qT[:, t0 - base:t1 - base], [qslot], t0 < S_LEN, t0)
            alv = (dbg or {}).get("alv", 9)
            for (t0, t1) in gl:
                if alv < 2:
                    break
                for tt in range(t0 // 128, t1 // 128):
                    lq = tt * 128 - base
                    if tt < 16:
                        loc = ([(tt - 1, C_TPREV)] if tt > 0 else []) + [(tt, None)] + ([(tt + 1, C_TNEXT)] if tt < 15 else [])
                    else:
                        loc = []
                    keys = loc + [(16, None), (17, None)]
                    po, pos_ = psring.get()
                    for hh in range(2):
                        pr = slice(hh * 64, (hh + 1) * 64)
                        PT, pts = pring.get()
                        nl = len(loc)
                        if nl:
                            pa, pas = psring.get()
                            for i, (kt, _) in enumerate(loc):
                                MM(pa[:, i * 128:(i + 1) * 128], [(kT[pr, kt * 128:(kt + 1) * 128], qT[pr, lq:lq + 128])],
                                   [ks[gidx_of_token(kt * 128)], qslot], [pas])
                            ACT(PT[:, 0:nl, :], pa[:, 0:nl * 128].rearrange("p (a b) -> p a b", a=nl), AF.Exp, [pas], [pts], scale=0.125)
                            for i, (kt, mk) in enumerate(loc):
                                if mk is not None:
                                    TT(PT[:, i, :], PT[:, i, :], cst[:, mk, :], ALU.mult, [pts, cst_slot], [pts])
                        pb, pbs = psring.get()
                        for i in range(2):
                            MM(pb[:, i * 128:(i + 1) * 128], [(kT[pr, (16 + i) * 128:(17 + i) * 128], qT[pr, lq:lq + 128])],
                               [ks[gidx_of_token(S_LEN)], qslot], [pbs])
                        ACT(PT[:, nl:nl + 2, :], pb[:, 0:256].rearrange("p (a b) -> p a b", a=2), AF.Exp, [pbs], [pts], scale=0.125)
                        if alv < 3:
                            continue
                        MM(po[:, hh * 65:(hh + 1) * 65], [(PT[:, i, :], Vaug[:, kt, hh, :]) for i, (kt, _) in enumerate(keys)],
                           [pts] + [vs[kt] for kt, _ in keys], [pos_])
                    if alv < 4:
                        continue
                    pov = po[:, 0:130].rearrange("p (h d) -> p h d", h=2)
                    den, dens = tring.get()
                    TT(den[:, 0:2], pov[:, :, 64], esk[:, c::4], ALU.add, [pos_, esk_slot], [dens])
                    RECIP(den[:, 0:2], dens)
                    atm, atms = bring.get()
                    TT(atm[:, 0:128].rearrange("p (h d) -> p h d", h=2), pov[:, :, 0:64],
                       den[:, 0:2].unsqueeze(2).to_broadcast([128, 2, 64]), ALU.mult, [pos_, dens], [atms])
                    if alv < 5:
                        continue
                    S.op("pe", lambda e, atm=atm: e.transpose(psb[:, 0:128], atm[:, 0:128], ident()), [atms, cst_slot], [psb_slot])
                    ACT(attnT[:, c, lq:lq + 128], psb[:, 0:128], AF.Copy, [psb_slot], [ats[c]])

    def stage_sgu(li, pi, last):
        gl = part_groups(pi, last, True)
        base = PARTS[pi][0]
        wts = [WDMA([128, 8, 256], WUVin[li, i], 2048) for i in range(4)]
        for (t0, t1) in gl:
            for tt in range(t0 // 128, t1 // 128):
                lq = tt * 128 - base
                hq = hpos(pi, tt * 128)
                pvs_, pvss = psring.get()
                pu, pus = psring.get()
                for half in range(2):
                    wv, wslot = wts[half]
                    MM(pvs_[:, half * 256:(half + 1) * 256], [(hT[:, k, hq:hq + 128], wv[:, k, :]) for k in range(8)], [wslot] + hs, [pvss])
                for half in range(2):
                    wv, wslot = wts[2 + half]
                    MM(pu[:, half * 256:(half + 1) * 256], [(hT[:, k, hq:hq + 128], wv[:, k, :]) for k in range(8)], [wslot] + hs, [pus])
                ug, ugs = bring.get()
                ACT(ug[:, :], pu[:, :], AF.Gelu_apprx_tanh, [pus], [ugs])
                vg, vgs = fring.get()
                ACT(vg[:, :], pvs_[:, :], AF.Gelu_apprx_tanh, [pvss], [vgs])
                vq, vqs = fring.get()
                ACT(vq[:, :], vg[:, :], AF.Square, [vgs], [vqs])
                st, sts = tring.get()
                S.op("dve", lambda e, st=st, vg=vg: e.reduce_sum(out=st[:, 0:1], in_=vg[:, :], axis=AX.X), [vgs], [sts])
                S.op("dve", lambda e, st=st, vq=vq: e.reduce_sum(out=st[:, 1:2], in_=vq[:, :], axis=AX.X), [vqs], [sts])
                TS(st[:, 2:3], st[:, 0:1], 1.0 / 512, ALU.mult, [sts], [sts])
                TT(st[:, 3:4], st[:, 2:3], st[:, 2:3], ALU.mult, [sts], [sts])
                STT(st[:, 4:5], st[:, 1:2], 1.0 / 512, st[:, 3:4], ALU.mult, ALU.subtract, [sts], [sts])
                ACT(st[:, 5:6], st[:, 4:5], AF.Sqrt, [sts], [sts], bias=EPS)
                RECIP(st[:, 5:6], sts)
                TS(vg[:, :], vg[:, :], st[:, 2:3], ALU.subtract, [vgs, sts], [vgs], s2=st[:, 5:6], op1=ALU.mult)
                TT(vg[:, :], vg[:, :], sgln[:, 0, :], ALU.mult, [vgs, sgln_slot], [vgs])
                vtm, vtms = bring.get()
                TT(vtm[:, :], vg[:, :], sgln[:, 1, :], ALU.add, [vgs, sgln_slot], [vtms])
                pm, pms = psring.get()
                for g in range(8):
                    MM(pm[:, g * 64:(g + 1) * 64], [(sguw[:, g, :], vtm[:, g * 64:(g + 1) * 64])], [sguw_slot, vtms], [pms])
                TT(vq[:, :].rearrange("p (g d) -> p g d", g=8), pm[:, :].rearrange("p (g d) -> p g d", g=8),
                   sm(li, O_SGUB, 8).unsqueeze(2).to_broadcast([128, 8, 64]), ALU.add, [pms, sm_slot], [vqs])
                ysg, ysgs = bring.get()
                TT(ysg[:, :], vq[:, :], ug[:, :], ALU.mult, [vqs, ugs], [ysgs])
                for cc in range(4):
                    S.op("pe", lambda e, ysg=ysg, cc=cc: e.transpose(psb[:, cc * 128:(cc + 1) * 128], ysg[:, cc * 128:(cc + 1) * 128], ident()),
                         [ysgs, cst_slot], [psb_slot])
                ACT(sguT[:, :, lq:lq + 128], psb[:, :].rearrange("p (c t) -> p c t", c=4), AF.Copy, [psb_slot], sgs)

    def stage_merge(li, pi, last):
        gl = part_groups(pi, last, True)
        base = PARTS[pi][0]
        for j in range(8):
            wg = [WDMA([128, 8, 128], WGin[li, j, b], 1024) for b in range(3)]
            wb, wbs = WDMA([128, 3, 4, 128], WBin[li, j], 1536)
            for (t0, t1) in gl:
                n = t1 - t0
                hp0 = hpos(pi, t0)
                gi = GIDX[(t0, t1)]
                l0 = t0 - base
                br = [(lambda k: cT[:, k, t0:t1], [cs[k][gi] for k in range(4)]),
                      (lambda k: attnT[:, k, l0:l0 + n], ats),
                      (lambda k: sguT[:, k, l0:l0 + n], sgs)]
                prods = []
                for b in range(3):
                    pg, pgs = psring.get()
                    MM(pg[:, 0:n], [(wg[b][0][:, k, :], hT[:, k, hp0:hp0 + n]) for k in range(8)], [wg[b][1]] + hs, [pgs])
                    sg, sgs_ = bring.get()
                    ACT(sg[:, 0:n], pg[:, 0:n], AF.Sigmoid, [pgs, sm_slot], [sgs_], bias=sm(li, O_GATEB + b * 8 + j))
                    py, pys = psring.get()
                    MM(py[:, 0:n], [(wb[:, b, k, :], br[b][0](k)) for k in range(4)], [wbs] + br[b][1], [pys])
                    pr_, prs = fring.get()
                    TT(pr_[:, 0:n], py[:, 0:n], sg[:, 0:n], ALU.mult, [pys, sgs_], [prs])
                    prods.append((pr_, prs))
                TT(prods[0][0][:, 0:n], prods[0][0][:, 0:n], prods[1][0][:, 0:n], ALU.add, [prods[0][1], prods[1][1]], [prods[0][1]])
                TT(mT[:, j, l0:l0 + n], prods[0][0][:, 0:n], prods[2][0][:, 0:n], ALU.add, [prods[0][1], prods[2][1]], [ms[j]])
        for i in range(8):
            wv, wslot = WDMA([128, 8, 128], WOin[li, i], 1024)
            for (t0, t1) in gl:
                n = t1 - t0
                l0 = t0 - base
                s_ = 1 if t0 >= S_LEN else 0
                gi = GIDX[(t0, t1)]
                po, pos_ = psring.get()
                MM(po[:, 0:n], [(wv[:, k, :], mT[:, k, l0:l0 + n]) for k in range(8)], [wslot] + ms, [pos_])
                STT(xT[:, i, t0:t1], po[:, 0:n], modT[:, 2, i, s_:s_ + 1], xT[:, i, t0:t1], ALU.mult, ALU.add,
                    [pos_, mod_slot, xs[i][gi]], [xs[i][gi]])

    def stage_ffn(li, pi, last):
        gl = part_groups(pi, last, True)
        base = PARTS[pi][0]
        for jj in range(22):
            wv, wslot = WDMA([128, 2, 8, 128], WUPin[li, jj], 2048)
            for (t0, t1) in gl:
                n = t1 - t0
                hp0 = hpos(pi, t0)
                l0 = t0 - base
                accs = []
                for ab in range(2):
                    ch = jj + 22 * ab
                    pz, pzs = psring.get()
                    MM(pz[:, 0:n + 2], [(wv[:, ab, k, :], hT[:, k, hp0 - 1:hp0 + n + 1]) for k in range(8)], [wslot] + hs, [pzs])
                    acc, accs_ = fring.get()
                    ACT(acc[:, 0:n], pz[:, 0:n], AF.Identity, [pzs, sm_slot], [accs_],
                        bias=sm(li, O_FFNB + ch), scale=sm(li, O_FFNW + ch * 3 + 0))
                    STT(acc[:, 0:n], pz[:, 1:n + 1], sm(li, O_FFNW + ch * 3 + 1), acc[:, 0:n], ALU.mult, ALU.add, [pzs, sm_slot, accs_], [accs_])
                    STT(acc[:, 0:n], pz[:, 2:n + 2], sm(li, O_FFNW + ch * 3 + 2), acc[:, 0:n], ALU.mult, ALU.add, [pzs, sm_slot, accs_], [accs_])
                    accs.append((acc, accs_))
                ACT(accs[0][0][:, 0:n], accs[0][0][:, 0:n], AF.Silu, [accs[0][1]], [accs[0][1]])
                TT(gT[:, jj, l0:l0 + n], accs[0][0][:, 0:n], accs[1][0][:, 0:n], ALU.mult, [accs[0][1], accs[1][1]], [gs[jj]])
        for i in range(8):
            w0, w0s = WDMA([128, 11, 128], WDNin[li, i, 0], 1408)
            w1, w1s = WDMA([128, 11, 128], WDNin[li, i, 1], 1408)
            for (t0, t1) in gl:
                n = t1 - t0
                l0 = t0 - base
                s_ = 1 if t0 >= S_LEN else 0
                gi = GIDX[(t0, t1)]
                po, pos_ = psring.get()
                MM(po[:, 0:n], [((w0 if k < 11 else w1)[:, k % 11, :], gT[:, k, l0:l0 + n]) for k in range(22)], [w0s, w1s] + gs, [pos_])
                STT(xT[:, i, t0:t1], po[:, 0:n], modT[:, 5, i, s_:s_ + 1], xT[:, i, t0:t1], ALU.mult, ALU.add,
                    [pos_, mod_slot, xs[i][gi]], [xs[i][gi]])

    def dump(name, ap, shape, dt, slots):
        o = nc.dram_tensor("dbg_" + name, list(shape), dt, kind="ExternalOutput").ap()
        d = S.op("sp", lambda e: e.dma_start(out=o, in_=ap), slots, [], dma=True)
        dbg_outs[name] = d

    stop = dbg.get("stop") if dbg else None
    done = False
    for li, lid in enumerate(layer_ids):
        last = lid == DEPTH - 1
        S.op("pool", lambda e, li=li: e.dma_start(out=sguw[:], in_=sguwin[li]), [], [sguw_slot], dma=True)
        PDMA(sgln[:], sglnin[li], [sgln_slot])
        for c in range(4):
            S.op("dve", lambda e, c=c: e.memset(yT[:, c, 0:16], 0.0), [], [ys[c]])
            S.op("dve", lambda e, c=c: e.memset(yT[:, c, 2064:2080], 0.0), [], [ys[c]])
            S.op("dve", lambda e, c=c: e.memset(yT[:, c, 2336:2352], 0.0), [], [ys[c]])
        stage_mod(li)
        if stop == "mod":
            break
        for pi in range(3):
            stage_norm(li, 0, pi, last, False)
            if stop == "norm1" and pi == 0:
                done = True
                break
            stage_proj_a(li, pi, last)
        if done or stop == "proja":
            break
        stage_conv(li, last)
        if stop == "conv":
            break
        S.barrier()
        for pi in range(3):
            stage_norm(li, 0, pi, last, False)
            stage_attn(li, pi, last)
            if stop == "attn" and pi == 0:
                done = True
                break
            stage_sgu(li, pi, last)
            if stop == "sgu" and pi == 0:
                done = True
                break
            stage_merge(li, pi, last)
        if done or stop == "merge":
            break
        S.barrier()
        for pi in (1, 2):
            norm_cols_at(li, 1, pi, PARTS[pi][0] - 1, 0, to_halo=pi)
        for pi in range(3):
            stage_norm(li, 1, pi, last, True)
            stage_ffn(li, pi, last)
        S.barrier()
        if stop == "layer0":
            break

    outs = []
    for k in range(8):
        outs.append(S.op("sp", lambda e, k=k: e.dma_start(out=xout[:, k, :], in_=xT[:, k, :]), xs[k], [], dma=True))
    if dbg:
        allslots = hs + ks + vs + ys + [s for r in cs for s in r] + ats + sgs + ms + [mod_slot, ab_slot]
        for name in dbg.get("dump", []):
            if name == "hT":
                dump("hT", hT[:], [128, 8, HW], BF16, allslots)
            elif name == "yT":
                dump("yT", yT, [128, 4, YW], BF16, allslots)
            elif name == "kT":
                dump("kT", kT[:], [128, T], BF16, allslots)
            elif name == "Vaug":
                dump("Vaug", Vaug[:], [128, 18, 2, 65], BF16, allslots)
            elif name == "cT":
                dump("cT", cT, [128, 4, T], BF16, allslots)
            elif name == "attnT":
                dump("attnT", attnT, [128, 4, 768], BF16, allslots)
            elif name == "sguT":
                dump("sguT", sguT, [128, 4, 768], BF16, allslots)
            elif name == "mT":
                dump("mT", mT, [128, 8, 768], BF16, allslots)
            elif name == "modT":
                dump("modT", modT[:], [128, 6, 8, 2], F32, allslots)
    S.alloc_sems(nc, es)
    with nc.Block() as block:
        S.emit(nc, block, {"sp": outs + list(dbg_outs.values())})
    es.close()
    return nc


def _tile_k(W):
    K, N = W.shape
    return np.ascontiguousarray(W.reshape(K // 128, 128, N).transpose(1, 0, 2))


def _consts():
    c = np.zeros((128, 7, 128), np.float32)
    c[:, C_ID, :] = np.eye(128, dtype=np.float32)
    c[:, C_RMS, :] = 1.0 / 1024
    c[:, C_LN, :] = 1.0 / 512
    for h in range(2):
        c[h * 64:(h + 1) * 64, C_BLK, h * 64:(h + 1) * 64] = 1.0 / 64
    for p in range(128):
        i = p % 32
        if i < 16:
            c[p + 16, C_ROT, p] = -1.0
        else:
            c[p - 16, C_ROT, p] = 1.0
    jj = np.arange(128)[:, None]
    r = np.arange(128)[None, :]
    c[:, C_TPREV, :] = (jj >= r).astype(np.float32)
    c[:, C_TNEXT, :] = (jj <= r).astype(np.float32)
    p = np.arange(128)
    d = p % 64
    axis = d // 32
    i = d % 16
    inv = np.power(np.float32(10000.0), -(i.astype(np.float32)) / np.float32(16)).astype(np.float32)
    t = np.arange(S_LEN)
    pos = np.where(axis[:, None] == 0, (t // 64)[None, :], (t % 64)[None, :]).astype(np.float32)
    ang = (pos * inv[:, None]).astype(np.float32)
    return c, np.cos(ang).astype(np.float32), np.sin(ang).astype(np.float32)


def _shared_weights(inp, layer_ids):
    out = {k: [] for k in ["ada", "sgln", "sguw", "WA", "WQ", "WUV", "WG", "WB", "WO", "WUP", "WDN", "smalls"]}
    for l in layer_ids:
        aw = _tile_k(np.asarray(inp["ada_w"][l]))
        out["ada"].append(aw.reshape(128, 8, 24, 2, 128).transpose(2, 0, 3, 1, 4))
        win = _tile_k(np.asarray(inp["w_in"][l]))

        def cols(a, n=128):
            return win[:, :, a:a + n]
        WA = np.stack([np.stack([cols(c * 128), cols(512 + c * 128)], 1) for c in range(4)]
                      + [np.stack([cols(1536), cols(1664)], 1)], 0)
        out["WA"].append(WA)
        WQ = np.stack([np.concatenate([cols(1024 + c * 64, 64), cols(1024 + (c + 4) * 64, 64)], 2) for c in range(4)], 0)
        out["WQ"].append(WQ)
        out["WUV"].append(np.stack([cols(2304, 256), cols(2560, 256), cols(1792, 256), cols(2048, 256)], 0))
        out["WG"].append(np.stack([np.stack([cols(2816 + b * 1024 + j * 128) for b in range(3)], 0) for j in range(8)], 0))
        perm = np.concatenate([np.concatenate([np.arange(c * 64, c * 64 + 64), np.arange((c + 4) * 64, (c + 4) * 64 + 64)]) for c in range(4)])
        bw = [_tile_k(np.asarray(inp["conv_out"][l])), _tile_k(np.asarray(inp["attn_out"][l])[perm]), _tile_k(np.asarray(inp["sgu_out"][l]))]
        out["WB"].append(np.stack([np.stack([bw[b][:, :, j * 128:(j + 1) * 128] for b in range(3)], 1) for j in range(8)], 0))
        wo = _tile_k(np.asarray(inp["w_o"][l]))
        out["WO"].append(np.stack([wo[:, :, i * 128:(i + 1) * 128] for i in range(8)], 0))
        up = _tile_k(np.asarray(inp["ffn_up"][l]))
        out["WUP"].append(np.stack([np.stack([up[:, :, jj * 128:(jj + 1) * 128], up[:, :, 2816 + jj * 128:2816 + (jj + 1) * 128]], 1) for jj in range(22)], 0))
        dn = _tile_k(np.asarray(inp["ffn_down"][l]))
        out["WDN"].append(np.stack([np.stack([dn[:, 0:11, i * 128:(i + 1) * 128], dn[:, 11:22, i * 128:(i + 1) * 128]], 0) for i in range(8)], 0))
        out["sguw"].append(np.asarray(inp["sgu_w"][l]).transpose(2, 0, 1))
        out["sgln"].append(np.stack([np.broadcast_to(np.asarray(inp["sgu_ln_g"][l]), (128, 512)),
                                     np.broadcast_to(np.asarray(inp["sgu_ln_b"][l]), (128, 512))], 1))
        sm = np.zeros((128, NS), np.float32)

        def fm(v):
            v = np.asarray(v)
            return v.reshape(-1, 128).T
        sm[:, O_ADAB:O_ADAB + 48] = fm(inp["ada_b"][l])
        sm[:, O_G1:O_G1 + 8] = fm(inp["norm1_g"][l])
        sm[:, O_G2:O_G2 + 8] = fm(inp["norm2_g"][l])
        sm[:, O_GATEB:O_GATEB + 24] = fm(np.asarray(inp["gate_b"][l]).reshape(-1))
        cw = np.asarray(inp["conv_dw_w"][l])
        sm[:, O_CONVW:O_CONVW + 124] = cw.reshape(31, 4, 128).transpose(2, 1, 0).reshape(128, 124)
        sm[:, O_CONVB:O_CONVB + 4] = fm(inp["conv_dw_b"][l])
        sm[:, O_CLNG:O_CLNG + 4] = fm(inp["conv_ln_g"][l])
        sm[:, O_CLNB:O_CLNB + 4] = fm(inp["conv_ln_b"][l])
        sm[:, O_QG] = np.tile(np.asarray(inp["q_norm_g"][l]), 2)
        sm[:, O_KG] = np.tile(np.asarray(inp["k_norm_g"][l]), 2)
        sm[:, O_SINK:O_SINK + 8] = np.broadcast_to(np.asarray(inp["attn_sink"][l]), (128, 8))
        sm[:, O_SGUB:O_SGUB + 8] = np.asarray(inp["sgu_b"][l]).T
        fw = np.asarray(inp["ffn_dw_w"][l])
        sm[:, O_FFNW:O_FFNW + 132] = fw.reshape(3, 44, 128).transpose(2, 1, 0).reshape(128, 132)
        sm[:, O_FFNB:O_FFNB + 44] = fm(inp["ffn_dw_b"][l])
        out["smalls"].append(sm)
    return {k: np.ascontiguousarray(np.stack(v, 0), dtype=np.float32) for k, v in out.items()}


_PROG_CACHE = {}


def _run(layer_ids, xT_list, inp, shared=None):
    key = tuple(layer_ids)
    if key not in _PROG_CACHE:
        _PROG_CACHE[key] = build_program(list(layer_ids))
    nc = _PROG_CACHE[key]
    if shared is None:
        shared = _shared_weights(inp, layer_ids)
    cst, cos, sin = _consts()
    c = np.asarray(inp["c"])
    c_ctx = np.asarray(inp["c_ctx"])
    in_maps = []
    for b in range(8):
        cc = np.stack([c[b], c_ctx], 1).reshape(8, 128, 2).transpose(1, 0, 2)
        m = dict(shared)
        m.update({"xT_in": xT_list[b], "cc": np.ascontiguousarray(cc, dtype=np.float32),
                  "consts": cst, "cos": cos, "sin": sin})
        in_maps.append(m)
    res = run_bass_kernel_spmd(nc, in_maps, core_ids=list(range(8)))
    return [r["xT_out"] for r in res.results]


def _to_xT(x, ctx, b):
    xc = np.concatenate([np.asarray(x[b]), np.asarray(ctx[b])], 0)
    return np.ascontiguousarray(xc.reshape(T, 8, 128).transpose(2, 1, 0), dtype=np.float32)


FUSED = True


def kernel(**inp):
    x = np.asarray(inp["x"])
    ctx = np.asarray(inp["ctx"])
    xT = [_to_xT(x, ctx, b) for b in range(8)]
    if FUSED:
        xT = _run([0, 1], xT, inp)
    else:
        for l in range(DEPTH):
            xT = _run([l], xT, inp)
    out = np.stack([np.asarray(o)[:, :, 0:S_LEN].transpose(2, 1, 0).reshape(S_LEN, D) for o in xT], 0)
    return out.astype(np.float32)
```

```python
import numpy as np
from contextlib import ExitStack
import concourse.bass as bass
import concourse.mybir as mybir
from concourse.bass_utils import run_bass_kernel_spmd

F32 = mybir.dt.float32
BF16 = mybir.dt.bfloat16
AF = mybir.ActivationFunctionType
ALU = mybir.AluOpType
AX = mybir.AxisListType

D = 1024
S_LEN = 2048
CL = 256
T = S_LEN + CL
DEPTH = 2
EPS = 1e-6
NS = 418
PARTS = [(0, 768), (768, 1536), (1536, 2304)]
HW = 772
YW = 2352
NDS = 8
NWS = 5

O_ADAB, O_G1, O_G2, O_GATEB, O_CONVW, O_CONVB, O_CLNG, O_CLNB = 0, 48, 56, 64, 88, 212, 216, 220
O_QG, O_KG, O_SINK, O_SGUB, O_FFNW, O_FFNB = 224, 225, 226, 234, 242, 374
C_ID, C_RMS, C_LN, C_BLK, C_ROT, C_TPREV, C_TNEXT = 0, 1, 2, 3, 4, 5, 6


def groups_of(p0, p1):
    out = []
    t = p0
    while t < p1:
        lim = p1 if (t >= S_LEN or p1 <= S_LEN) else S_LEN
        n = min(384, lim - t)
        out.append((t, t + n))
        t += n
    return out


ALL_GROUPS = [g for p in PARTS for g in groups_of(*p)]
GIDX = {g: i for i, g in enumerate(ALL_GROUPS)}


def gidx_of_token(t):
    for i, (a, b) in enumerate(ALL_GROUPS):
        if a <= t < b:
            return i
    raise ValueError(t)


def hpos(pi, t):
    p0, _ = PARTS[pi]
    if pi < 2:
        return t - p0 + 1
    return (t - 1536 + 1) if t < S_LEN else (t - S_LEN + 515)


def ypos(t):
    return t + 16 if t < S_LEN else t + 32


class Slot:
    __slots__ = ("w", "r")

    def __init__(self):
        self.w = {}
        self.r = {}


class Op:
    __slots__ = ("eng", "fn", "dma", "sig", "id", "deps", "dk", "semval", "dsem", "dval", "ep", "tag")


class Sched:
    ENG = ["pe", "act", "dve", "pool", "sp"]

    def __init__(self):
        self.ops = {e: [] for e in self.ENG}
        self.dma_hist = {e: [] for e in self.ENG}
        self.last = {e: None for e in self.ENG}
        self.barrier_ops = []
        self.dma_since_barrier = []
        self.nid = 0
        self.epoch = 0

    def op(self, eng, fn, reads=(), writes=(), dma=False):
        o = Op()
        o.eng, o.fn, o.dma, o.sig, o.id = eng, fn, dma, False, self.nid
        o.ep = self.epoch
        o.tag = getattr(self, "tag", "")
        self.nid += 1
        deps = {}
        key = ("dma", o.id) if dma else eng

        def add(p, raw):
            if (not p.dma) and (not dma) and p.eng == eng:
                if (not raw) or eng == "pe":
                    return
            deps[p.id] = p

        for s in reads:
            for p in s.w.values():
                add(p, True)
        for s in writes:
            for p in s.w.values():
                add(p, False)
            for p in s.r.values():
                add(p, False)
        for p in self.barrier_ops:
            add(p, True)
        if dma:
            k = len(self.dma_hist[eng])
            if k >= NDS:
                prev = self.dma_hist[eng][k - NDS]
                deps[prev.id] = prev
            o.dk = k
            self.dma_hist[eng].append(o)
            self.dma_since_barrier.append(o)
        o.deps = list(deps.values())
        for p in o.deps:
            p.sig = True
        wset = set(id(s) for s in writes)
        for s in writes:
            if s.r:
                s.w = {key: o}
                s.r = {}
            else:
                s.w[key] = o
        for s in reads:
            if id(s) not in wset:
                s.r[key] = o
        self.ops[eng].append(o)
        if not dma:
            self.last[eng] = o
        return o

    def barrier(self):
        self.barrier_ops = [o for o in self.last.values() if o is not None] + list(self.dma_since_barrier)
        self.dma_since_barrier = []
        self.epoch += 1

    def alloc_sems(self, nc, es):
        self.csem = {}
        self.dsems = {}
        for e in self.ENG:
            if self.ops[e]:
                for ep in range(self.epoch + 1):
                    self.csem[(e, ep)] = es.enter_context(nc.semaphore("c_%s%d" % (e, ep)))
                if self.dma_hist[e]:
                    self.dsems[e] = [es.enter_context(nc.semaphore("d_%s%d" % (e, i))) for i in range(NDS)]

    def emit(self, nc, block, final_waits):
        csem = self.csem
        dsems = self.dsems
        for e in self.ENG:
            c = {}
            for o in self.ops[e]:
                if o.dma:
                    o.dsem = dsems[e][o.dk % NDS]
                    o.dval = 16 * (o.dk // NDS + 1)
                elif o.sig:
                    c[o.ep] = c.get(o.ep, 0) + 1
                    o.semval = c[o.ep]

        def run(e, h):
            known = {}

            def wait_for(p):
                if p.dma:
                    sem, val, key = p.dsem, p.dval, ("d", p.eng, p.dk % NDS)
                else:
                    sem, val, key = csem[(p.eng, p.ep)], p.semval, ("c", p.eng, p.ep)
                if known.get(key, 0) >= val:
                    return
                h.wait_ge(sem, val)
                known[key] = val

            for o in self.ops[e]:
                for p in o.deps:
                    wait_for(p)
                ins = o.fn(h)
                if o.dma:
                    ins.then_inc(o.dsem, 16)
                elif o.sig:
                    ins.then_inc(csem[(e, o.ep)], 1)
            for p in final_waits.get(e, []):
                wait_for(p)

        hooks = {"pe": block.tensor, "act": block.scalar, "dve": block.vector, "pool": block.gpsimd, "sp": block.sync}
        for e in self.ENG:
            if self.ops[e]:
                hooks[e](lambda h, e=e: run(e, h))


class Ring:
    def __init__(self, aps, slots=None):
        self.aps = aps
        self.slots = slots if slots is not None else [Slot() for _ in aps]
        self.i = 0
        self.pinned = set()
        self.last_idx = None

    def get(self):
        while self.i in self.pinned:
            self.i = (self.i + 1) % len(self.aps)
        i = self.i
        self.i = (i + 1) % len(self.aps)
        self.last_idx = i
        return self.aps[i], self.slots[i]


def build_program(layer_ids, dbg=None):
    L = len(layer_ids)
    nc = bass.Bass("TRN2", target_bir_lowering=False)
    S = Sched()

    def din(name, shape):
        return nc.dram_tensor(name, list(shape), F32, kind="ExternalInput").ap()

    xin = din("xT_in", [128, 8, T])
    ccin = din("cc", [128, 8, 2])
    adain = din("ada", [L, 24, 128, 2, 8, 128])
    smin = din("smalls", [L, 128, NS])
    sglnin = din("sgln", [L, 128, 2, 512])
    sguwin = din("sguw", [L, 128, 8, 128])
    cstin = din("consts", [128, 7, 128])
    cosin = din("cos", [128, S_LEN])
    sinin = din("sin", [128, S_LEN])
    WAin = din("WA", [L, 5, 128, 2, 8, 128])
    WQin = din("WQ", [L, 4, 128, 8, 128])
    WUVin = din("WUV", [L, 4, 128, 8, 256])
    WGin = din("WG", [L, 8, 3, 128, 8, 128])
    WBin = din("WB", [L, 8, 128, 3, 4, 128])
    WOin = din("WO", [L, 8, 128, 8, 128])
    WUPin = din("WUP", [L, 22, 128, 2, 8, 128])
    WDNin = din("WDN", [L, 8, 2, 128, 11, 128])
    xout = nc.dram_tensor("xT_out", [128, 8, T], F32, kind="ExternalOutput").ap()
    dbg_outs = {}

    es = ExitStack()

    def sb(name, shape, dt):
        return es.enter_context(nc.sbuf_tensor(name, list(shape), dt))

    xT = sb("xT", [128, 8, T], F32)
    hT = sb("hT", [128, 8, HW], BF16)
    kT = sb("kT", [128, T], BF16)
    Vaug = sb("Vaug", [128, 18, 2, 65], BF16)
    CT_W = 4 * T
    X_W = 4 * YW + 31 * 128
    assert X_W >= 4 * 768 * 2 + 8 * 768
    AR_W = CT_W + X_W
    assert AR_W >= 22 * 768
    arena = sb("arena", [128, AR_W], BF16)
    cT = arena[:, 0:CT_W].rearrange("p (c t) -> p c t", c=4)
    yT = arena[:, CT_W:CT_W + 4 * YW].rearrange("p (c t) -> p c t", c=4)
    Dg = arena[:, CT_W + 4 * YW:CT_W + 4 * YW + 31 * 128].rearrange("p (k c) -> p k c", k=31)
    attnT = arena[:, CT_W:CT_W + 4 * 768].rearrange("p (c t) -> p c t", c=4)
    sguT = arena[:, CT_W + 4 * 768:CT_W + 8 * 768].rearrange("p (c t) -> p c t", c=4)
    mT = arena[:, CT_W + 8 * 768:CT_W + 16 * 768].rearrange("p (c t) -> p c t", c=8)
    gT = arena[:, 0:22 * 768].rearrange("p (c t) -> p c t", c=22)

    wsl = sb("wsl", [128, NWS, 2048], BF16)
    wring = Ring([wsl[:, i, :] for i in range(NWS)])
    Ft = sb("Ft", [128, 6, 512], F32)
    fring = Ring([Ft[:, i, :] for i in range(6)])
    Rt = sb("Rt", [128, 2, 384], F32)
    rring = Ring([Rt[:, i, :] for i in range(2)])
    Bt = sb("Bt", [128, 6, 512], BF16)
    bring = Ring([Bt[:, i, :] for i in range(6)])
    sqb = sb("sqb", [128, 8, 384], BF16)
    sq_slot = Slot()
    qTc = sb("qTc", [128, 2, 768], BF16)
    qring = Ring([qTc[:, i, :] for i in range(2)])
    PTt = sb("PT", [128, 4, 5, 128], BF16)
    pring = Ring([PTt[:, i, :, :] for i in range(4)])
    cst = sb("cst", [128, 7, 128], BF16)
    smalls = sb("smalls_sb", [128, L, NS], F32)
    sgln = sb("sgln_sb", [128, 2, 512], F32)
    sguw = sb("sguw_sb", [128, 8, 128], BF16)
    cct = sb("cct", [128, 8, 2], F32)
    sct = sb("sct", [128, 8, 2], BF16)
    modT = sb("modT", [128, 6, 8, 2], F32)
    AB = sb("AB", [128, 2, 8, 2], F32)
    esk = sb("esk", [128, 8], F32)
    halo_buf = sb("halo_buf", [128, 8, 4], BF16)
    halo_slot = Slot()
    tiny = sb("tiny", [128, 8, 8], F32)
    tring = Ring([tiny[:, i, :] for i in range(8)])

    ps = [es.enter_context(nc.psum_tensor("ps%d" % i, [128, 512], F32)) for i in range(7)]
    psring = Ring([p[:, :] for p in ps])
    sring = Ring(psring.aps[0:4], psring.slots[0:4])
    oring = Ring(psring.aps[4:7], psring.slots[4:7])
    psb = es.enter_context(nc.psum_tensor("psb", [128, 512], BF16))
    psb_slot = Slot()

    xs = [[Slot() for _ in ALL_GROUPS] for _ in range(8)]
    hs = [Slot() for _ in range(8)]
    ks = [Slot() for _ in ALL_GROUPS]
    vs = [Slot() for _ in range(18)]
    ys = [Slot() for _ in range(4)]
    cs = [[Slot() for _ in ALL_GROUPS] for _ in range(4)]
    dg_slot = Slot()
    ats = [Slot() for _ in range(4)]
    sgs = [Slot() for _ in range(4)]
    ms = [Slot() for _ in range(8)]
    gs = [Slot() for _ in range(22)]
    cst_slot, sm_slot, sgln_slot, sguw_slot = Slot(), Slot(), Slot(), Slot()
    cc_slot, sct_slot, mod_slot, ab_slot, esk_slot = Slot(), Slot(), Slot(), Slot(), Slot()

    def sm(li, off, n=1):
        return smalls[:, li, off:off + n]

    def ACT(out, in_, func, r, w, bias=None, scale=None):
        kw = {}
        if bias is not None:
            kw["bias"] = bias
        if scale is not None:
            kw["scale"] = scale
        return S.op("act", lambda e: e.activation(out=out, in_=in_, func=func, **kw), r, w)

    def TT(out, in0, in1, op, r, w):
        return S.op("dve", lambda e: e.tensor_tensor(out=out, in0=in0, in1=in1, op=op), r, w)

    def TS(out, in0, s1, op0, r, w, s2=None, op1=None):
        if op1 is None:
            return S.op("dve", lambda e: e.tensor_scalar(out=out, in0=in0, scalar1=s1, scalar2=None, op0=op0), r, w)
        return S.op("dve", lambda e: e.tensor_scalar(out=out, in0=in0, scalar1=s1, scalar2=s2, op0=op0, op1=op1), r, w)

    def STT(out, in0, scalar, in1, op0, op1, r, w):
        return S.op("dve", lambda e: e.scalar_tensor_tensor(out=out, in0=in0, scalar=scalar, in1=in1, op0=op0, op1=op1), r, w)

    def RECIP(ap, slot):
        return S.op("dve", lambda e: e.reciprocal(out=ap, in_=ap), [slot], [slot])

    def MM(out, pairs, r, w):
        n = len(pairs)
        o = None
        for i, (l, rh) in enumerate(pairs):
            o = S.op("pe", lambda e, l=l, rh=rh, i=i: e.matmul(out, l, rh, start=(i == 0), stop=(i == n - 1)), r, w)
        return o

    def WDMA(view_shape, src, nelem):
        ap, slot = wring.get()
        v = ap[:, 0:nelem]
        if len(view_shape) == 3:
            v = v.rearrange("p (a b) -> p a b", a=view_shape[1])
        elif len(view_shape) == 4:
            v = v.rearrange("p (a b c) -> p a b c", a=view_shape[1], b=view_shape[2])
        S.op("pool", lambda e: e.dma_start(out=v, in_=src), [], [slot], dma=True)
        return v, slot

    def PDMA(out, src, w):
        return S.op("sp", lambda e: e.dma_start(out=out, in_=src), [], w, dma=True)

    for k in range(8):
        PDMA(xT[:, k, :], xin[:, k, :], xs[k])
    S.op("pool", lambda e: e.dma_start(out=cst[:], in_=cstin), [], [cst_slot], dma=True)
    PDMA(smalls[:], smin.rearrange("l p n -> p l n"), [sm_slot])
    PDMA(cct[:], ccin, [cc_slot])
    ACT(sct[:], cct[:], AF.Silu, [cc_slot], [sct_slot])
    S.op("pool", lambda e: e.memset(Vaug[:, :, :, 64:65], 1.0), [], vs)

    def ident():
        return cst[:, C_ID, :]

    def stage_mod(li):
        pst, pslot = psring.get()
        for jg in range(24):
            wv, wslot = WDMA([128, 2, 8, 128], adain[li, jg], 2048)
            for jj in range(2):
                j = jg * 2 + jj
                MM(pst[:, j * 2:(j + 1) * 2], [(wv[:, jj, k, :], sct[:, k, :]) for k in range(8)], [wslot, sct_slot], [pslot])
        pv = pst[:, 0:96].rearrange("p (j s) -> p j s", s=2)
        mv = modT[:].rearrange("p i k s -> p (i k) s")
        for s_ in range(2):
            TT(mv[:, :, s_], pv[:, :, s_], sm(li, O_ADAB, 48), ALU.add, [pslot, sm_slot], [mod_slot])
        for which, (mi, og) in enumerate([(1, O_G1), (4, O_G2)]):
            for s_ in range(2):
                STT(AB[:, which, :, s_], modT[:, mi, :, s_], 1.0, sm(li, og, 8), ALU.add, ALU.mult, [mod_slot, sm_slot], [ab_slot])
        ACT(esk[:], sm(li, O_SINK, 8), AF.Exp, [sm_slot], [esk_slot])

    def norm_cols(li, which, pi, t0, t1):
        n = t1 - t0
        s_ = 1 if t0 >= S_LEN else 0
        hp0 = hpos(pi, t0)
        rs = []
        a = t0
        while a < t1:
            g = gidx_of_token(a)
            rs.append(g)
            a = ALL_GROUPS[g][1]
        xr = [xs[k][g] for k in range(8) for g in rs]
        ACT(sqb[:, :, 0:n], xT[:, :, t0:t1], AF.Square, xr, [sq_slot])
        pst, pslot = psring.get()
        MM(pst[:, 0:n], [(cst[:, C_RMS, :], sqb[:, k, 0:n]) for k in range(8)], [sq_slot, cst_slot], [pslot])
        sd, sdslot = rring.get()
        ACT(sd[:, 0:n], pst[:, 0:n], AF.Sqrt, [pslot], [sdslot], bias=EPS)
        RECIP(sd[:, 0:n], sdslot)
        shift_i = 0 if which == 0 else 3
        for k in range(8):
            tmp, tslot = fring.get()
            STT(tmp[:, 0:n], xT[:, k, t0:t1], AB[:, which, k, s_:s_ + 1], sd[:, 0:n], ALU.mult, ALU.mult,
                [xs[k][g] for g in rs] + [ab_slot, sdslot], [tslot])
            ACT(hT[:, k, hp0:hp0 + n], tmp[:, 0:n], AF.Identity, [tslot, mod_slot], [hs[k]],
                bias=modT[:, shift_i, k, s_:s_ + 1])

    def part_groups(pi, li_is_last, phase_b):
        gl = groups_of(*PARTS[pi])
        if phase_b and li_is_last:
            gl = [g for g in gl if g[0] < S_LEN]
        return gl

    def stage_norm(li, which, pi, last, with_halo):
        gl = groups_of(*PARTS[pi])
        if last and which == 1:
            gl = [g for g in gl if g[0] < S_LEN]
        for (t0, t1) in gl:
            norm_cols(li, which, pi, t0, t1)
        if with_halo:
            def zero_col(c):
                S.op("dve", lambda e: e.memset(hT[:, :, c:c + 1], 0.0), [], hs)
            p0, p1 = PARTS[pi]
            x1 = min(p1, S_LEN)
            if p0 > 0:
                S.op("dve", lambda e, pi=pi: e.tensor_copy(out=hT[:, :, 0:1], in_=halo_buf[:, :, pi:pi + 1]), [halo_slot], hs)
            else:
                zero_col(0)
            if x1 < S_LEN:
                save = hpos(pi, x1 - 1) + 1
                norm_cols_at(li, which, pi, x1, save)
            else:
                zero_col(hpos(pi, x1 - 1) + 1)
            if pi == 2:
                zero_col(514)
                zero_col(771)

    def norm_cols_at(li, which, pi, t, col, to_halo=None):
        g = gidx_of_token(t)
        xr = [xs[k][g] for k in range(8)]
        ACT(sqb[:, :, 0:1], xT[:, :, t:t + 1], AF.Square, xr, [sq_slot])
        pst, pslot = psring.get()
        MM(pst[:, 0:1], [(cst[:, C_RMS, :], sqb[:, k, 0:1]) for k in range(8)], [sq_slot, cst_slot], [pslot])
        sd, sdslot = rring.get()
        ACT(sd[:, 0:1], pst[:, 0:1], AF.Sqrt, [pslot], [sdslot], bias=EPS)
        RECIP(sd[:, 0:1], sdslot)
        shift_i = 0 if which == 0 else 3
        for k in range(8):
            tmp, tslot = fring.get()
            STT(tmp[:, 0:1], xT[:, k, t:t + 1], AB[:, which, k, 0:1], sd[:, 0:1], ALU.mult, ALU.mult,
                [xs[k][g], ab_slot, sdslot], [tslot])
            if to_halo is None:
                ACT(hT[:, k, col:col + 1], tmp[:, 0:1], AF.Identity, [tslot, mod_slot], [hs[k]],
                    bias=modT[:, shift_i, k, 0:1])
            else:
                ACT(halo_buf[:, k, to_halo:to_halo + 1], tmp[:, 0:1], AF.Identity, [tslot, mod_slot], [halo_slot],
                    bias=modT[:, shift_i, k, 0:1])

    def qknorm_rope(li, praw, pslot, n, goff, dst, dslots, rope, t0):
        sq, sqs = bring.get()
        ACT(sq[:, 0:n], praw[:, 0:n], AF.Square, [pslot], [sqs])
        p2, p2s = psring.get()
        MM(p2[:, 0:n], [(cst[:, C_BLK, :], sq[:, 0:n])], [sqs, cst_slot], [p2s])
        sd, sds = rring.get()
        ACT(sd[:, 0:n], p2[:, 0:n], AF.Sqrt, [p2s], [sds], bias=EPS)
        RECIP(sd[:, 0:n], sds)
        if not rope:
            STT(dst, praw[:, 0:n], sm(li, goff), sd[:, 0:n], ALU.mult, ALU.mult, [pslot, sm_slot, sds], dslots)
            return
        qn, qns = bring.get()
        STT(qn[:, 0:n], praw[:, 0:n], sm(li, goff), sd[:, 0:n], ALU.mult, ALU.mult, [pslot, sm_slot, sds], [qns])
        p3, p3s = psring.get()
        MM(p3[:, 0:n], [(cst[:, C_ROT, :], qn[:, 0:n])], [qns, cst_slot], [p3s])
        cs_, css = fring.get()
        sn_, sns = fring.get()
        PDMA(cs_[:, 0:n], cosin[:, t0:t0 + n], [css])
        PDMA(sn_[:, 0:n], sinin[:, t0:t0 + n], [sns])
        TT(cs_[:, 0:n], qn[:, 0:n], cs_[:, 0:n], ALU.mult, [qns, css], [css])
        TT(sn_[:, 0:n], p3[:, 0:n], sn_[:, 0:n], ALU.mult, [p3s, sns], [sns])
        TT(dst, cs_[:, 0:n], sn_[:, 0:n], ALU.add, [css, sns], dslots)

    def stage_proj_a(li, pi, last):
        gl = groups_of(*PARTS[pi])
        for c in range(4):
            wv, wslot = WDMA([128, 2, 8, 128], WAin[li, c], 2048)
            for (t0, t1) in gl:
                if last and t0 >= S_LEN:
                    continue
                n = t1 - t0
                hp0 = hpos(pi, t0)
                pa, pas = psring.get()
                pb, pbs = psring.get()
                MM(pa[:, 0:n], [(wv[:, 0, k, :], hT[:, k, hp0:hp0 + n]) for k in range(8)], [wslot] + hs, [pas])
                MM(pb[:, 0:n], [(wv[:, 1, k, :], hT[:, k, hp0:hp0 + n]) for k in range(8)], [wslot] + hs, [pbs])
                sg, sgs_ = fring.get()
                ACT(sg[:, 0:n], pb[:, 0:n], AF.Sigmoid, [pbs], [sgs_])
                yp0 = ypos(t0)
                TT(yT[:, c, yp0:yp0 + n], pa[:, 0:n], sg[:, 0:n], ALU.mult, [pas, sgs_], [ys[c]])
        wv, wslot = WDMA([128, 2, 8, 128], WAin[li, 4], 2048)
        for (t0, t1) in gl:
            n = t1 - t0
            hp0 = hpos(pi, t0)
            gi = GIDX[(t0, t1)]
            pk, pks = psring.get()
            MM(pk[:, 0:n], [(wv[:, 0, k, :], hT[:, k, hp0:hp0 + n]) for k in range(8)], [wslot] + hs, [pks])
            qknorm_rope(li, pk, pks, n, O_KG, kT[:, t0:t1], [ks[gi]], t0 < S_LEN, t0)
            for tt in range(t0 // 128, t1 // 128):
                hq = hpos(pi, tt * 128)
                pv, pvs = psring.get()
                MM(pv[:, 0:128], [(hT[:, k, hq:hq + 128], wv[:, 1, k, :]) for k in range(8)], [wslot] + hs, [pvs])
                ACT(Vaug[:, tt, :, 0:64], pv[:, 0:128].rearrange("p (g d) -> p g d", g=2), AF.Copy, [pvs], [vs[tt]])

    def stage_conv(li, last):
        gl = [g for g in ALL_GROUPS if not (last and g[0] >= S_LEN)]
        for c in range(4):
            for k in range(31):
                TS(Dg[:, k, :], ident(), sm(li, O_CONVW + c * 31 + k), ALU.mult, [cst_slot, sm_slot], [dg_slot])
            for (t0, t1) in gl:
                n = t1 - t0
                yp0 = ypos(t0)
                pc, pcs = psring.get()
                MM(pc[:, 0:n], [(Dg[:, k, :], yT[:, c, yp0 + k - 15:yp0 + k - 15 + n]) for k in range(31)],
                   [dg_slot, ys[c]], [pcs])
                ACT(cT[:, c, t0:t1], pc[:, 0:n], AF.Identity, [pcs, sm_slot], [cs[c][GIDX[(t0, t1)]]],
                    bias=sm(li, O_CONVB + c))
        for (t0, t1) in gl:
            n = t1 - t0
            gi = GIDX[(t0, t1)]
            cr = [cs[c][gi] for c in range(4)]
            ACT(sqb[:, 0:4, 0:n], cT[:, :, t0:t1], AF.Square, cr, [sq_slot])
            pm, pms = psring.get()
            pe_, pes = psring.get()
            MM(pm[:, 0:n], [(cst[:, C_LN, :], cT[:, c, t0:t1]) for c in range(4)], cr + [cst_slot], [pms])
            MM(pe_[:, 0:n], [(cst[:, C_LN, :], sqb[:, c, 0:n]) for c in range(4)], [sq_slot, cst_slot], [pes])
            mS, mSs = fring.get()
            vr, vrs = fring.get()
            ACT(mS[:, 0:n], pm[:, 0:n], AF.Copy, [pms], [mSs])
            ACT(vr[:, 0:n], pm[:, 0:n], AF.Square, [pms], [vrs])
            TT(vr[:, 0:n], pe_[:, 0:n], vr[:, 0:n], ALU.subtract, [pes, vrs], [vrs])
            ACT(vr[:, 0:n], vr[:, 0:n], AF.Sqrt, [vrs], [vrs], bias=EPS)
            RECIP(vr[:, 0:n], vrs)
            for c in range(4):
                t_, ts_ = fring.get()
                TT(t_[:, 0:n], cT[:, c, t0:t1], mS[:, 0:n], ALU.subtract, [cs[c][gi], mSs], [ts_])
                TT(t_[:, 0:n], t_[:, 0:n], vr[:, 0:n], ALU.mult, [ts_, vrs], [ts_])
                ACT(cT[:, c, t0:t1], t_[:, 0:n], AF.Silu, [ts_, sm_slot], [cs[c][gi]],
                    bias=sm(li, O_CLNB + c), scale=sm(li, O_CLNG + c))

    def stage_attn(li, pi, last):
        gl = part_groups(pi, last, True)
        base = PARTS[pi][0]
        for c in range(4):
            wv, wslot = WDMA([128, 8, 128], WQin[li, c], 1024)
            qT, qslot = qring.get()
            for (t0, t1) in gl:
                n = t1 - t0
                hp0 = hpos(pi, t0)
                pq, pqs = psring.get()
                MM(pq[:, 0:n], [(wv[:, k, :], hT[:, k, hp0:hp0 + n]) for k in range(8)], [wslot] + hs, [pqs])
                qknorm_rope(li, pq, pqs, n, O_QG, qT[:, t0 - base:t1 - base], [qslot], t0 < S_LEN, t0)
            tiles = [tt for (t0, t1) in gl for tt in range(t0 // 128, t1 // 128)]
            st = {}

            def front(tt):
                lq = tt * 128 - base
                if tt < 16:
                    loc = ([(tt - 1, C_TPREV)] if tt > 0 else []) + [(tt, None)] + ([(tt + 1, C_TNEXT)] if tt < 15 else [])
                else:
                    loc = []
                keys = loc + [(16, None), (17, None)]
                nl = len(loc)
                pts_l = []
                for hh in range(2):
                    pr = slice(hh * 64, (hh + 1) * 64)
                    PT, pts = pring.get()
                    if nl:
                        pa, pas = sring.get()
                        for i, (kt, _) in enumerate(loc):
                            MM(pa[:, i * 128:(i + 1) * 128], [(kT[pr, kt * 128:(kt + 1) * 128], qT[pr, lq:lq + 128])],
                               [ks[gidx_of_token(kt * 128)], qslot], [pas])
                        ACT(PT[:, 0:nl, :], pa[:, 0:nl * 128].rearrange("p (a b) -> p a b", a=nl), AF.Exp, [pas], [pts], scale=0.125)
                        for i, (kt, mk) in enumerate(loc):
                            if mk is not None:
                                TT(PT[:, i, :], PT[:, i, :], cst[:, mk, :], ALU.mult, [pts, cst_slot], [pts])
                    pb, pbs = sring.get()
                    for i in range(2):
                        MM(pb[:, i * 128:(i + 1) * 128], [(kT[pr, (16 + i) * 128:(17 + i) * 128], qT[pr, lq:lq + 128])],
                           [ks[gidx_of_token(S_LEN)], qslot], [pbs])
                    ACT(PT[:, nl:nl + 2, :], pb[:, 0:256].rearrange("p (a b) -> p a b", a=2), AF.Exp, [pbs], [pts], scale=0.125)
                    pts_l.append((PT, pts))
                st[tt] = [keys, pts_l, None]

            def back1(tt):
                keys, pts_l, _ = st[tt]
                po, pos_ = oring.get()
                for hh in range(2):
                    PT, pts = pts_l[hh]
                    MM(po[:, hh * 65:(hh + 1) * 65], [(PT[:, i, :], Vaug[:, kt, hh, :]) for i, (kt, _) in enumerate(keys)],
                       [pts] + [vs[kt] for kt, _ in keys], [pos_])
                pov = po[:, 0:130].rearrange("p (h d) -> p h d", h=2)
                den, dens = tring.get()
                TT(den[:, 0:2], pov[:, :, 64], esk[:, c::4], ALU.add, [pos_, esk_slot], [dens])
                RECIP(den[:, 0:2], dens)
                atm, atms = bring.get()
                TT(atm[:, 0:128].rearrange("p (h d) -> p h d", h=2), pov[:, :, 0:64],
                   den[:, 0:2].unsqueeze(2).to_broadcast([128, 2, 64]), ALU.mult, [pos_, dens], [atms])
                st[tt][2] = (atm, atms)

            def back2(tt):
                lq = tt * 128 - base
                atm, atms = st[tt][2]
                S.op("pe", lambda e, atm=atm: e.transpose(psb[:, 0:128], atm[:, 0:128], ident()), [atms, cst_slot], [psb_slot])
                ACT(attnT[:, c, lq:lq + 128], psb[:, 0:128], AF.Copy, [psb_slot], [ats[c]])
                del st[tt]

            nt = len(tiles)
            for i in range(nt + 2):
                if i < nt:
                    front(tiles[i])
                if 0 <= i - 1 < nt:
                    back1(tiles[i - 1])
                if 0 <= i - 2 < nt:
                    back2(tiles[i - 2])

    def stage_sgu(li, pi, last):
        gl = part_groups(pi, last, True)
        base = PARTS[pi][0]
        wts = []
        pins = []
        for i in range(4):
            wts.append(WDMA([128, 8, 256], WUVin[li, i], 2048))
            pins.append(wring.last_idx)
            wring.pinned.add(wring.last_idx)
        for (t0, t1) in gl:
            for tt in range(t0 // 128, t1 // 128):
                lq = tt * 128 - base
                hq = hpos(pi, tt * 128)
                pvs_, pvss = psring.get()
                pu, pus = psring.get()
                for half in range(2):
                    wv, wslot = wts[half]
                    MM(pvs_[:, half * 256:(half + 1) * 256], [(hT[:, k, hq:hq + 128], wv[:, k, :]) for k in range(8)], [wslot] + hs, [pvss])
                for half in range(2):
                    wv, wslot = wts[2 + half]
                    MM(pu[:, half * 256:(half + 1) * 256], [(hT[:, k, hq:hq + 128], wv[:, k, :]) for k in range(8)], [wslot] + hs, [pus])
                ug, ugs = bring.get()
                ACT(ug[:, :], pu[:, :], AF.Gelu_apprx_tanh, [pus], [ugs])
                vg, vgs = fring.get()
                ACT(vg[:, :], pvs_[:, :], AF.Gelu_apprx_tanh, [pvss], [vgs])
                vq, vqs = fring.get()
                ACT(vq[:, :], vg[:, :], AF.Square, [vgs], [vqs])
                st, sts = tring.get()
                S.op("dve", lambda e, st=st, vg=vg: e.reduce_sum(out=st[:, 0:1], in_=vg[:, :], axis=AX.X), [vgs], [sts])
                S.op("dve", lambda e, st=st, vq=vq: e.reduce_sum(out=st[:, 1:2], in_=vq[:, :], axis=AX.X), [vqs], [sts])
                TS(st[:, 2:3], st[:, 0:1], 1.0 / 512, ALU.mult, [sts], [sts])
                TT(st[:, 3:4], st[:, 2:3], st[:, 2:3], ALU.mult, [sts], [sts])
                STT(st[:, 4:5], st[:, 1:2], 1.0 / 512, st[:, 3:4], ALU.mult, ALU.subtract, [sts], [sts])
                ACT(st[:, 5:6], st[:, 4:5], AF.Sqrt, [sts], [sts], bias=EPS)
                RECIP(st[:, 5:6], sts)
                TS(vg[:, :], vg[:, :], st[:, 2:3], ALU.subtract, [vgs, sts], [vgs], s2=st[:, 5:6], op1=ALU.mult)
                TT(vg[:, :], vg[:, :], sgln[:, 0, :], ALU.mult, [vgs, sgln_slot], [vgs])
                vtm, vtms = bring.get()
                TT(vtm[:, :], vg[:, :], sgln[:, 1, :], ALU.add, [vgs, sgln_slot], [vtms])
                pm, pms = psring.get()
                for g in range(8):
                    MM(pm[:, g * 64:(g + 1) * 64], [(sguw[:, g, :], vtm[:, g * 64:(g + 1) * 64])], [sguw_slot, vtms], [pms])
                TT(vq[:, :].rearrange("p (g d) -> p g d", g=8), pm[:, :].rearrange("p (g d) -> p g d", g=8),
                   sm(li, O_SGUB, 8).unsqueeze(2).to_broadcast([128, 8, 64]), ALU.add, [pms, sm_slot], [vqs])
                ysg, ysgs = bring.get()
                TT(ysg[:, :], vq[:, :], ug[:, :], ALU.mult, [vqs, ugs], [ysgs])
                for cc in range(4):
                    S.op("pe", lambda e, ysg=ysg, cc=cc: e.transpose(psb[:, cc * 128:(cc + 1) * 128], ysg[:, cc * 128:(cc + 1) * 128], ident()),
                         [ysgs, cst_slot], [psb_slot])
                ACT(sguT[:, :, lq:lq + 128], psb[:, :].rearrange("p (c t) -> p c t", c=4), AF.Copy, [psb_slot], sgs)
                yield
        for i in pins:
            wring.pinned.discard(i)

    def stage_merge(li, pi, last):
        gl = part_groups(pi, last, True)
        base = PARTS[pi][0]
        for j in range(8):
            wg = [WDMA([128, 8, 128], WGin[li, j, b], 1024) for b in range(3)]
            wb, wbs = WDMA([128, 3, 4, 128], WBin[li, j], 1536)
            for (t0, t1) in gl:
                n = t1 - t0
                hp0 = hpos(pi, t0)
                gi = GIDX[(t0, t1)]
                l0 = t0 - base
                br = [(lambda k: cT[:, k, t0:t1], [cs[k][gi] for k in range(4)]),
                      (lambda k: attnT[:, k, l0:l0 + n], ats),
                      (lambda k: sguT[:, k, l0:l0 + n], sgs)]
                prods = []
                for b in range(3):
                    pg, pgs = psring.get()
                    MM(pg[:, 0:n], [(wg[b][0][:, k, :], hT[:, k, hp0:hp0 + n]) for k in range(8)], [wg[b][1]] + hs, [pgs])
                    sg, sgs_ = bring.get()
                    ACT(sg[:, 0:n], pg[:, 0:n], AF.Sigmoid, [pgs, sm_slot], [sgs_], bias=sm(li, O_GATEB + b * 8 + j))
                    py, pys = psring.get()
                    MM(py[:, 0:n], [(wb[:, b, k, :], br[b][0](k)) for k in range(4)], [wbs] + br[b][1], [pys])
                    pr_, prs = fring.get()
                    TT(pr_[:, 0:n], py[:, 0:n], sg[:, 0:n], ALU.mult, [pys, sgs_], [prs])
                    prods.append((pr_, prs))
                TT(prods[0][0][:, 0:n], prods[0][0][:, 0:n], prods[1][0][:, 0:n], ALU.add, [prods[0][1], prods[1][1]], [prods[0][1]])
                TT(mT[:, j, l0:l0 + n], prods[0][0][:, 0:n], prods[2][0][:, 0:n], ALU.add, [prods[0][1], prods[2][1]], [ms[j]])
        for i in range(8):
            wv, wslot = WDMA([128, 8, 128], WOin[li, i], 1024)
            for (t0, t1) in gl:
                n = t1 - t0
                l0 = t0 - base
                s_ = 1 if t0 >= S_LEN else 0
                gi = GIDX[(t0, t1)]
                po, pos_ = psring.get()
                MM(po[:, 0:n], [(wv[:, k, :], mT[:, k, l0:l0 + n]) for k in range(8)], [wslot] + ms, [pos_])
                STT(xT[:, i, t0:t1], po[:, 0:n], modT[:, 2, i, s_:s_ + 1], xT[:, i, t0:t1], ALU.mult, ALU.add,
                    [pos_, mod_slot, xs[i][gi]], [xs[i][gi]])

    def stage_ffn(li, pi, last):
        gl = part_groups(pi, last, True)
        base = PARTS[pi][0]
        for jj in range(22):
            wv, wslot = WDMA([128, 2, 8, 128], WUPin[li, jj], 2048)
            for (t0, t1) in gl:
                n = t1 - t0
                hp0 = hpos(pi, t0)
                l0 = t0 - base
                accs = []
                pzl = []
                for ab in range(2):
                    pz, pzs = psring.get()
                    MM(pz[:, 0:n + 2], [(wv[:, ab, k, :], hT[:, k, hp0 - 1:hp0 + n + 1]) for k in range(8)], [wslot] + hs, [pzs])
                    pzl.append((pz, pzs))
                for ab in range(2):
                    ch = jj + 22 * ab
                    pz, pzs = pzl[ab]
                    acc, accs_ = fring.get()
                    ACT(acc[:, 0:n], pz[:, 0:n], AF.Identity, [pzs, sm_slot], [accs_],
                        bias=sm(li, O_FFNB + ch), scale=sm(li, O_FFNW + ch * 3 + 0))
                    accs.append((acc, accs_))
                for tap in (1, 2):
                    for ab in range(2):
                        ch = jj + 22 * ab
                        pz, pzs = pzl[ab]
                        acc, accs_ = accs[ab]
                        STT(acc[:, 0:n], pz[:, tap:n + tap], sm(li, O_FFNW + ch * 3 + tap), acc[:, 0:n], ALU.mult, ALU.add, [pzs, sm_slot, accs_], [accs_])
                ACT(accs[0][0][:, 0:n], accs[0][0][:, 0:n], AF.Silu, [accs[0][1]], [accs[0][1]])
                TT(gT[:, jj, l0:l0 + n], accs[0][0][:, 0:n], accs[1][0][:, 0:n], ALU.mult, [accs[0][1], accs[1][1]], [gs[jj]])
        for i in range(8):
            w0, w0s = WDMA([128, 11, 128], WDNin[li, i, 0], 1408)
            w1, w1s = WDMA([128, 11, 128], WDNin[li, i, 1], 1408)
            for (t0, t1) in gl:
                n = t1 - t0
                l0 = t0 - base
                s_ = 1 if t0 >= S_LEN else 0
                gi = GIDX[(t0, t1)]
                po, pos_ = psring.get()
                MM(po[:, 0:n], [((w0 if k < 11 else w1)[:, k % 11, :], gT[:, k, l0:l0 + n]) for k in range(22)], [w0s, w1s] + gs, [pos_])
                STT(xT[:, i, t0:t1], po[:, 0:n], modT[:, 5, i, s_:s_ + 1], xT[:, i, t0:t1], ALU.mult, ALU.add,
                    [pos_, mod_slot, xs[i][gi]], [xs[i][gi]])

    def interleave(ga, gb, ratio):
        da = db = False
        while not (da and db):
            if not db:
                try:
                    next(gb)
                except StopIteration:
                    db = True
            for _ in range(ratio):
                if not da:
                    try:
                        next(ga)
                    except StopIteration:
                        da = True

    def dump(name, ap, shape, dt, slots):
        o = nc.dram_tensor("dbg_" + name, list(shape), dt, kind="ExternalOutput").ap()
        d = S.op("sp", lambda e: e.dma_start(out=o, in_=ap), slots, [], dma=True)
        dbg_outs[name] = d

    stop = dbg.get("stop") if dbg else None
    done = False
    for li, lid in enumerate(layer_ids):
        last = lid == DEPTH - 1
        S.op("pool", lambda e, li=li: e.dma_start(out=sguw[:], in_=sguwin[li]), [], [sguw_slot], dma=True)
        PDMA(sgln[:], sglnin[li], [sgln_slot])
        for c in range(4):
            S.op("dve", lambda e, c=c: e.memset(yT[:, c, 0:16], 0.0), [], [ys[c]])
            S.op("dve", lambda e, c=c: e.memset(yT[:, c, 2064:2080], 0.0), [], [ys[c]])
            S.op("dve", lambda e, c=c: e.memset(yT[:, c, 2336:2352], 0.0), [], [ys[c]])
        S.tag = 'stage_mod'
        stage_mod(li)
        if stop == "mod":
            break
        for pi in range(3):
            S.tag = 'normA'
            stage_norm(li, 0, pi, last, False)
            S.tag = 'projA'
            if stop == "norm1" and pi == 0:
                done = True
                break
            stage_proj_a(li, pi, last)
        if done or stop == "proja":
            break
        S.tag = 'stage_conv'
        stage_conv(li, last)
        if stop == "conv":
            break
        S.barrier()
        for pi in range(3):
            S.tag = 'normB'
            stage_norm(li, 0, pi, last, False)
            S.tag = 'attn_sgu'
            stage_attn(li, pi, last)
            if stop == "attn":
                done = True
                break
            for _ in stage_sgu(li, pi, last):
                pass
            if stop == "sgu" and pi == 0:
                done = True
                break
            S.tag = 'merge'
            stage_merge(li, pi, last)
        if done or stop == "merge":
            break
        S.barrier()
        for pi in (1, 2):
            norm_cols_at(li, 1, pi, PARTS[pi][0] - 1, 0, to_halo=pi)
        for pi in range(3):
            S.tag = 'norm2'
            stage_norm(li, 1, pi, last, True)
            S.tag = 'ffn'
            stage_ffn(li, pi, last)
        S.barrier()
        if stop == "layer0":
            break

    outs = []
    for k in range(8):
        outs.append(S.op("sp", lambda e, k=k: e.dma_start(out=xout[:, k, :], in_=xT[:, k, :]), xs[k], [], dma=True))
    if dbg:
        allslots = hs + ks + vs + ys + [s for r in cs for s in r] + ats + sgs + ms + [mod_slot, ab_slot]
        for name in dbg.get("dump", []):
            if name == "hT":
                dump("hT", hT[:], [128, 8, HW], BF16, allslots)
            elif name == "yT":
                dump("yT", yT, [128, 4, YW], BF16, allslots)
            elif name == "kT":
                dump("kT", kT[:], [128, T], BF16, allslots)
            elif name == "Vaug":
                dump("Vaug", Vaug[:], [128, 18, 2, 65], BF16, allslots)
            elif name == "cT":
                dump("cT", cT, [128, 4, T], BF16, allslots)
            elif name == "attnT":
                dump("attnT", attnT, [128, 4, 768], BF16, allslots)
            elif name == "sguT":
                dump("sguT", sguT, [128, 4, 768], BF16, allslots)
            elif name == "mT":
                dump("mT", mT, [128, 8, 768], BF16, allslots)
            elif name == "modT":
                dump("modT", modT[:], [128, 6, 8, 2], F32, allslots)
    S.alloc_sems(nc, es)
    nc._sched = S
    with nc.Block() as block:
        S.emit(nc, block, {"sp": outs + list(dbg_outs.values())})
    es.close()
    return nc


def _tile_k(W):
    K, N = W.shape
    return np.ascontiguousarray(W.reshape(K // 128, 128, N).transpose(1, 0, 2))


def _consts():
    c = np.zeros((128, 7, 128), np.float32)
    c[:, C_ID, :] = np.eye(128, dtype=np.float32)
    c[:, C_RMS, :] = 1.0 / 1024
    c[:, C_LN, :] = 1.0 / 512
    for h in range(2):
        c[h * 64:(h + 1) * 64, C_BLK, h * 64:(h + 1) * 64] = 1.0 / 64
    for p in range(128):
        i = p % 32
        if i < 16:
            c[p + 16, C_ROT, p] = -1.0
        else:
            c[p - 16, C_ROT, p] = 1.0
    jj = np.arange(128)[:, None]
    r = np.arange(128)[None, :]
    c[:, C_TPREV, :] = (jj >= r).astype(np.float32)
    c[:, C_TNEXT, :] = (jj <= r).astype(np.float32)
    p = np.arange(128)
    d = p % 64
    axis = d // 32
    i = d % 16
    inv = np.power(np.float32(10000.0), -(i.astype(np.float32)) / np.float32(16)).astype(np.float32)
    t = np.arange(S_LEN)
    pos = np.where(axis[:, None] == 0, (t // 64)[None, :], (t % 64)[None, :]).astype(np.float32)
    ang = (pos * inv[:, None]).astype(np.float32)
    return c, np.cos(ang).astype(np.float32), np.sin(ang).astype(np.float32)


def _shared_weights(inp, layer_ids):
    out = {k: [] for k in ["ada", "sgln", "sguw", "WA", "WQ", "WUV", "WG", "WB", "WO", "WUP", "WDN", "smalls"]}
    for l in layer_ids:
        aw = _tile_k(np.asarray(inp["ada_w"][l]))
        out["ada"].append(aw.reshape(128, 8, 24, 2, 128).transpose(2, 0, 3, 1, 4))
        win = _tile_k(np.asarray(inp["w_in"][l]))

        def cols(a, n=128):
            return win[:, :, a:a + n]
        WA = np.stack([np.stack([cols(c * 128), cols(512 + c * 128)], 1) for c in range(4)]
                      + [np.stack([cols(1536), cols(1664)], 1)], 0)
        out["WA"].append(WA)
        WQ = np.stack([np.concatenate([cols(1024 + c * 64, 64), cols(1024 + (c + 4) * 64, 64)], 2) for c in range(4)], 0)
        out["WQ"].append(WQ)
        out["WUV"].append(np.stack([cols(2304, 256), cols(2560, 256), cols(1792, 256), cols(2048, 256)], 0))
        out["WG"].append(np.stack([np.stack([cols(2816 + b * 1024 + j * 128) for b in range(3)], 0) for j in range(8)], 0))
        perm = np.concatenate([np.concatenate([np.arange(c * 64, c * 64 + 64), np.arange((c + 4) * 64, (c + 4) * 64 + 64)]) for c in range(4)])
        bw = [_tile_k(np.asarray(inp["conv_out"][l])), _tile_k(np.asarray(inp["attn_out"][l])[perm]), _tile_k(np.asarray(inp["sgu_out"][l]))]
        out["WB"].append(np.stack([np.stack([bw[b][:, :, j * 128:(j + 1) * 128] for b in range(3)], 1) for j in range(8)], 0))
        wo = _tile_k(np.asarray(inp["w_o"][l]))
        out["WO"].append(np.stack([wo[:, :, i * 128:(i + 1) * 128] for i in range(8)], 0))
        up = _tile_k(np.asarray(inp["ffn_up"][l]))
        out["WUP"].append(np.stack([np.stack([up[:, :, jj * 128:(jj + 1) * 128], up[:, :, 2816 + jj * 128:2816 + (jj + 1) * 128]], 1) for jj in range(22)], 0))
        dn = _tile_k(np.asarray(inp["ffn_down"][l]))
        out["WDN"].append(np.stack([np.stack([dn[:, 0:11, i * 128:(i + 1) * 128], dn[:, 11:22, i * 128:(i + 1) * 128]], 0) for i in range(8)], 0))
        out["sguw"].append(np.asarray(inp["sgu_w"][l]).transpose(2, 0, 1))
        out["sgln"].append(np.stack([np.broadcast_to(np.asarray(inp["sgu_ln_g"][l]), (128, 512)),
                                     np.broadcast_to(np.asarray(inp["sgu_ln_b"][l]), (128, 512))], 1))
        sm = np.zeros((128, NS), np.float32)

        def fm(v):
            v = np.asarray(v)
            return v.reshape(-1, 128).T
        sm[:, O_ADAB:O_ADAB + 48] = fm(inp["ada_b"][l])
        sm[:, O_G1:O_G1 + 8] = fm(inp["norm1_g"][l])
        sm[:, O_G2:O_G2 + 8] = fm(inp["norm2_g"][l])
        sm[:, O_GATEB:O_GATEB + 24] = fm(np.asarray(inp["gate_b"][l]).reshape(-1))
        cw = np.asarray(inp["conv_dw_w"][l])
        sm[:, O_CONVW:O_CONVW + 124] = cw.reshape(31, 4, 128).transpose(2, 1, 0).reshape(128, 124)
        sm[:, O_CONVB:O_CONVB + 4] = fm(inp["conv_dw_b"][l])
        sm[:, O_CLNG:O_CLNG + 4] = fm(inp["conv_ln_g"][l])
        sm[:, O_CLNB:O_CLNB + 4] = fm(inp["conv_ln_b"][l])
        sm[:, O_QG] = np.tile(np.asarray(inp["q_norm_g"][l]), 2)
        sm[:, O_KG] = np.tile(np.asarray(inp["k_norm_g"][l]), 2)
        sm[:, O_SINK:O_SINK + 8] = np.broadcast_to(np.asarray(inp["attn_sink"][l]), (128, 8))
        sm[:, O_SGUB:O_SGUB + 8] = np.asarray(inp["sgu_b"][l]).T
        fw = np.asarray(inp["ffn_dw_w"][l])
        sm[:, O_FFNW:O_FFNW + 132] = fw.reshape(3, 44, 128).transpose(2, 1, 0).reshape(128, 132)
        sm[:, O_FFNB:O_FFNB + 44] = fm(inp["ffn_dw_b"][l])
        out["smalls"].append(sm)
    return {k: np.ascontiguousarray(np.stack(v, 0), dtype=np.float32) for k, v in out.items()}


_PROG_CACHE = {}


def _run(layer_ids, xT_list, inp, shared=None):
    key = tuple(layer_ids)
    if key not in _PROG_CACHE:
        _PROG_CACHE[key] = build_program(list(layer_ids))
    nc = _PROG_CACHE[key]
    if shared is None:
        shared = _shared_weights(inp, layer_ids)
    cst, cos, sin = _consts()
    c = np.asarray(inp["c"])
    c_ctx = np.asarray(inp["c_ctx"])
    in_maps = []
    for b in range(8):
        cc = np.stack([c[b], c_ctx], 1).reshape(8, 128, 2).transpose(1, 0, 2)
        m = dict(shared)
        m.update({"xT_in": xT_list[b], "cc": np.ascontiguousarray(cc, dtype=np.float32),
                  "consts": cst, "cos": cos, "sin": sin})
        in_maps.append(m)
    res = run_bass_kernel_spmd(nc, in_maps, core_ids=list(range(8)))
    return [r["xT_out"] for r in res.results]


def _to_xT(x, ctx, b):
    xc = np.concatenate([np.asarray(x[b]), np.asarray(ctx[b])], 0)
    return np.ascontiguousarray(xc.reshape(T, 8, 128).transpose(2, 1, 0), dtype=np.float32)


FUSED = True


def kernel(**inp):
    x = np.asarray(inp["x"])
    ctx = np.asarray(inp["ctx"])
    xT = [_to_xT(x, ctx, b) for b in range(8)]
    if FUSED:
        xT = _run([0, 1], xT, inp)
    else:
        for l in range(DEPTH):
            xT = _run([l], xT, inp)
    out = np.stack([np.asarray(o)[:, :, 0:S_LEN].transpose(2, 1, 0).reshape(S_LEN, D) for o in xT], 0)
    return out.astype(np.float32)
```
